# Optimizing a Trainium2 kernel written in Bass

```python
import jax, jax.numpy as jnp
from jax import lax
import numpy as np

D_MODEL = 1024
BATCH = 4
SEQ = 8192
DEPTH = 4
DEC_BATCH = 8
DEC_SEQ = 64
PAST_LEN = 2048

CHUNK = 64
N_A = DEPTH // 2
N_B = DEPTH - N_A
A_CHUNK = 128
A_WIDTH = D_MODEL
A_GROUPS = 8
A_GROUP_DIM = A_WIDTH // A_GROUPS
N_HEADS = 16
QK_NOPE = 64
QK_ROPE = 32
V_DIM = 64
KV_LORA = 128
Q_LORA = 256
D_FF = 2816
ROPE_BASE = 10000.0
EPS = 1e-6
Q_BLOCK = 128
ATT_SCALE = (QK_NOPE + QK_ROPE) ** -0.5

kernel_name = 'yoco_gmlp_mla_macaron_stream_step'


def rmsnorm(x, g):
    x32 = x.astype(jnp.float32)
    y = x32 * lax.rsqrt(jnp.mean(x32 * x32, axis=-1, keepdims=True) + EPS)
    return (y * g.astype(jnp.float32)).astype(x.dtype)


def swiglu(x, w_gu, w_down):
    g, u = jnp.split(x @ w_gu, 2, axis=-1)
    return (jax.nn.silu(g) * u) @ w_down


def rope(x, pos):
    half = QK_ROPE // 2
    inv = 1.0 / (ROPE_BASE ** (jnp.arange(half, dtype=jnp.float32) * (2.0 / QK_ROPE)))
    ang = pos.astype(jnp.float32)[:, None] * inv[None, :]
    shape = (ang.shape[0],) + (1,) * (x.ndim - 3) + (half,)
    cos = jnp.cos(ang).reshape(shape)
    sin = jnp.sin(ang).reshape(shape)
    x32 = x.astype(jnp.float32)
    x1, x2 = x32[..., :half], x32[..., half:]
    return jnp.concatenate([x1 * cos - x2 * sin, x2 * cos + x1 * sin], axis=-1).astype(x.dtype)


def spatial_gate(v, w_s, b_s):
    L = v.shape[-3]
    mask = jnp.tril(jnp.ones((L, L), dtype=bool))
    w = jnp.where(mask, w_s[:, :L, :L], jnp.zeros((), w_s.dtype))
    return jnp.einsum('gts,...sgd->...tgd', w, v) + b_s[:, :L].T[:, :, None]


def gmlp_mixer(h, w_in, v_g, w_s, b_s, w_out, chunked):
    z = jax.nn.gelu(h @ w_in, approximate=False)
    u, v = jnp.split(z, 2, axis=-1)
    v = rmsnorm(v, v_g)
    bsz, s, _ = v.shape
    if chunked:
        vb = v.reshape(bsz, s // A_CHUNK, A_CHUNK, A_GROUPS, A_GROUP_DIM)
    else:
        vb = v.reshape(bsz, s, A_GROUPS, A_GROUP_DIM)
    sv = spatial_gate(vb, w_s, b_s).reshape(bsz, s, A_WIDTH)
    return (u * sv) @ w_out, v


def mla_latent(h, kv_g, w_dkv, ckv_g, pos):
    kv = rmsnorm(h, kv_g) @ w_dkv
    ckv = rmsnorm(kv[..., :KV_LORA], ckv_g)
    kr = rope(kv[..., KV_LORA:], pos)
    return ckv, kr


def mla_expand(ckv, w_uk, w_uv):
    bsz, s, _ = ckv.shape
    kn = (ckv @ w_uk).reshape(bsz, s, N_HEADS, QK_NOPE)
    vv = (ckv @ w_uv).reshape(bsz, s, N_HEADS, V_DIM)
    return kn, vv


def mla_queries(h, w_dq, q_g, w_uq, pos):
    bsz, s, _ = h.shape
    q = (rmsnorm(h @ w_dq, q_g) @ w_uq).reshape(bsz, s, N_HEADS, QK_NOPE + QK_ROPE)
    return q[..., :QK_NOPE], rope(q[..., QK_NOPE:], pos)


def attend(qn, qr, kn, kr, vv, mask):
    s = jnp.einsum('bqhd,bkhd->bhqk', qn, kn) + jnp.einsum('bqhr,bkr->bhqk', qr, kr)
    s = s.astype(jnp.float32) * ATT_SCALE
    if mask is not None:
        s = jnp.where(mask[None, None], s, -jnp.inf)
    p = jax.nn.softmax(s, axis=-1).astype(vv.dtype)
    return jnp.einsum('bhqk,bkhd->bqhd', p, vv)


def mla_prompt_attention(qn, qr, kn, kr, vv):
    bsz, s = qn.shape[:2]
    nqb = s // Q_BLOCK
    qn_b = qn.reshape(bsz, nqb, Q_BLOCK, N_HEADS, QK_NOPE).transpose(1, 0, 2, 3, 4)
    qr_b = qr.reshape(bsz, nqb, Q_BLOCK, N_HEADS, QK_ROPE).transpose(1, 0, 2, 3, 4)
    k_chunk = jnp.arange(s) // CHUNK

    def block(args):
        qn_i, qr_i, i = args
        q_chunk = (i * Q_BLOCK + jnp.arange(Q_BLOCK)) // CHUNK
        mask = k_chunk[None, :] <= q_chunk[:, None]
        return attend(qn_i, qr_i, kn, kr, vv, mask)

    o = lax.map(block, (qn_b, qr_b, jnp.arange(nqb)))
    return o.transpose(1, 0, 2, 3, 4).reshape(bsz, s, N_HEADS * V_DIM)


def setup_inputs(seed: int = 0) -> dict:
    key = jax.random.key(seed)
    ks = jax.random.split(key, 32)

    def nrm(k, shape, scale):
        return jax.random.normal(k, shape, jnp.float32) * scale

    def gain(k, shape):
        return 1.0 + 0.01 * jax.random.normal(k, shape, jnp.float32)

    return {
        'x_prompt': nrm(ks[0], (BATCH, SEQ, D_MODEL), 1.0),
        'x_sample': nrm(ks[1], (DEC_BATCH, DEC_SEQ, D_MODEL), 1.0),
        'cache_ckv': nrm(ks[2], (DEC_BATCH, PAST_LEN, KV_LORA), 1.0),
        'cache_krope': nrm(ks[3], (DEC_BATCH, PAST_LEN, QK_ROPE), 1.0),
        'ffn1_norm': gain(ks[4], (DEPTH, D_MODEL)),
        'ffn1_w_gu': nrm(ks[5], (DEPTH, D_MODEL, 2 * D_FF), D_MODEL ** -0.5),
        'ffn1_w_down': nrm(ks[6], (DEPTH, D_FF, D_MODEL), D_FF ** -0.5),
        'mix_norm': gain(ks[7], (DEPTH, D_MODEL)),
        'ffn2_norm': gain(ks[8], (DEPTH, D_MODEL)),
        'ffn2_w_gu': nrm(ks[9], (DEPTH, D_MODEL, 2 * D_FF), D_MODEL ** -0.5),
        'ffn2_w_down': nrm(ks[10], (DEPTH, D_FF, D_MODEL), D_FF ** -0.5),
        'a_w_in': nrm(ks[11], (N_A, D_MODEL, 2 * A_WIDTH), D_MODEL ** -0.5),
        'a_v_norm': gain(ks[12], (N_A, A_WIDTH)),
        'a_w_s': nrm(ks[13], (N_A, A_GROUPS, A_CHUNK, A_CHUNK), A_CHUNK ** -0.5),
        'a_b_s': 1.0 + 0.02 * jax.random.normal(ks[14], (N_A, A_GROUPS, A_CHUNK), jnp.float32),
        'a_w_out': nrm(ks[15], (N_A, A_WIDTH, D_MODEL), A_WIDTH ** -0.5),
        'kv_norm': gain(ks[16], (D_MODEL,)),
        'w_dkv': nrm(ks[17], (D_MODEL, KV_LORA + QK_ROPE), D_MODEL ** -0.5),
        'ckv_norm': gain(ks[18], (KV_LORA,)),
        'w_uk': nrm(ks[19], (KV_LORA, N_HEADS * QK_NOPE), KV_LORA ** -0.5),
        'w_uv': nrm(ks[20], (KV_LORA, N_HEADS * V_DIM), KV_LORA ** -0.5),
        'b_w_dq': nrm(ks[21], (N_B, D_MODEL, Q_LORA), D_MODEL ** -0.5),
        'b_q_norm': gain(ks[22], (N_B, Q_LORA)),
        'b_w_uq': nrm(ks[23], (N_B, Q_LORA, N_HEADS * (QK_NOPE + QK_ROPE)), Q_LORA ** -0.5),
        'b_w_o': nrm(ks[24], (N_B, N_HEADS * V_DIM, D_MODEL), (N_HEADS * V_DIM) ** -0.5),
        'final_norm': gain(ks[25], (D_MODEL,)),
    }


def reference(x_prompt, x_sample, cache_ckv, cache_krope,
              ffn1_norm, ffn1_w_gu, ffn1_w_down, mix_norm, ffn2_norm, ffn2_w_gu, ffn2_w_down,
              a_w_in, a_v_norm, a_w_s, a_b_s, a_w_out,
              kv_norm, w_dkv, ckv_norm, w_uk, w_uv,
              b_w_dq, b_q_norm, b_w_uq, b_w_o, final_norm):

    def run(x, pos, is_prompt, past_ckv, past_krope):
        h = x
        a_v_rows = []
        ckv_new = kr_new = kn = kr_all = vv = None
        for l in range(DEPTH):
            h = h + 0.5 * swiglu(rmsnorm(h, ffn1_norm[l]), ffn1_w_gu[l], ffn1_w_down[l])
            hn = rmsnorm(h, mix_norm[l])
            if l < N_A:
                y, v_rows = gmlp_mixer(hn, a_w_in[l], a_v_norm[l], a_w_s[l], a_b_s[l], a_w_out[l], is_prompt)
                a_v_rows.append(v_rows)
            else:
                j = l - N_A
                qn, qr = mla_queries(hn, b_w_dq[j], b_q_norm[j], b_w_uq[j], pos)
                if is_prompt:
                    o = mla_prompt_attention(qn, qr, kn, kr_all, vv)
                else:
                    o = attend(qn, qr, kn, kr_all, vv, None).reshape(x.shape[0], x.shape[1], N_HEADS * V_DIM)
                y = o @ b_w_o[j]
            h = h + y
            h = h + 0.5 * swiglu(rmsnorm(h, ffn2_norm[l]), ffn2_w_gu[l], ffn2_w_down[l])
            if l == N_A - 1:
                ckv_new, kr_new = mla_latent(h, kv_norm, w_dkv, ckv_norm, pos)
                if is_prompt:
                    ckv_all, kr_all = ckv_new, kr_new
                else:
                    ckv_all = jnp.concatenate([past_ckv, ckv_new], axis=1)
                    kr_all = jnp.concatenate([past_krope, kr_new], axis=1)
                kn, vv = mla_expand(ckv_all, w_uk, w_uv)
        return rmsnorm(h, final_norm), ckv_new, kr_new, a_v_rows

    pos_p = jnp.arange(x_prompt.shape[1])
    y_prompt, new_ckv_prompt, new_krope_prompt, _ = run(x_prompt, pos_p, True, None, None)

    pos_s = PAST_LEN + jnp.arange(x_sample.shape[1])
    y_sample, new_ckv_sample, new_krope_sample, a_rows = run(x_sample, pos_s, False, cache_ckv, cache_krope)
    new_a_v_sample = jnp.stack(a_rows, axis=0)

    return (y_prompt, y_sample, new_ckv_prompt, new_krope_prompt, new_ckv_sample, new_krope_sample, new_a_v_sample)
```

```python
import math
from contextlib import ExitStack
import numpy as np
import concourse.bass as bass
import concourse.mybir as mybir
from concourse.bass_utils import run_bass_kernel_spmd

F32 = mybir.dt.float32
BF16 = mybir.dt.bfloat16
AF = mybir.ActivationFunctionType
ALU = mybir.AluOpType

D = 1024
DEPTH = 4
N_A = 2
DFF = 2816
NJ = DFF // 128
KVL = 128
ROPE = 32
NH = 16
QL = 256
EPS = 1e-6
ATT_SCALE = 96 ** -0.5
SLOT = 6144
NSLOT = 3
KVB = 16
SEM_LIMIT = 30000
SAME_ENGINE_SKIP = 1 << 30
import os as _os0
INLINE_WAIT = _os0.environ.get("KINLINE", "0") == "1"


class Buf:
    __slots__ = ("name", "w", "r", "excl")

    def __init__(self, name, excl=False):
        self.name = name
        self.w = None
        self.r = []
        self.excl = excl


class Op:
    __slots__ = ("eng", "pos", "fn", "deps", "needs_inc", "key", "kidx", "gen", "count")


class Sched:
    ENGS = ("pe", "act", "dve", "pool", "sp")

    def __init__(self, nc):
        self.nc = nc
        self.ops = {e: [] for e in self.ENGS}
        self.seen = {e: {} for e in self.ENGS}
        self.keycount = {}
        self.nops = 0

    def op(self, eng, fn, reads=(), writes=(), key=None):
        o = Op()
        o.eng = eng
        o.pos = len(self.ops[eng])
        o.fn = fn
        o.needs_inc = False
        o.key = key
        o.gen = 0
        o.count = 0
        if key is not None:
            self.keycount[key] = self.keycount.get(key, 0) + 1
            o.kidx = self.keycount[key]
        else:
            o.kidx = 0
        writes = list(writes) + [b for b in reads if b.excl]
        reads = [b for b in reads if not b.excl]
        cand = []
        wset = set(id(b) for b in writes)
        for b in reads:
            if id(b) in wset:
                continue
            if b.w is not None:
                cand.append(b.w)
        for b in writes:
            if b.w is not None:
                cand.append(b.w)
            cand.extend(b.r)
        seen = self.seen[eng]
        deps = {}
        for p in cand:
            if p.key is not None:
                stream, idx = ("k", p.key), p.kidx
            else:
                stream, idx = ("e", p.eng), p.pos
                if p.eng == eng and key is None:
                    if eng == "pe":
                        continue
                    if o.pos - p.pos >= SAME_ENGINE_SKIP:
                        continue
            if seen.get(stream, -1) >= idx:
                continue
            if stream not in deps or (deps[stream].kidx if p.key is not None else deps[stream].pos) < idx:
                deps[stream] = p
        for stream, p in deps.items():
            seen[stream] = p.kidx if p.key is not None else p.pos
            p.needs_inc = True
        o.deps = list(deps.values())
        for b in reads:
            if id(b) not in wset:
                b.r.append(o)
        for b in writes:
            b.w = o
            b.r = []
        self.ops[eng].append(o)
        self.nops += 1
        return o

    def emit(self, es, final_waits):
        nc = self.nc
        esems = {}
        for e in self.ENGS:
            gen, cnt = 0, 0
            esems[e] = [es.enter_context(nc.semaphore("s_%s_0" % e))]
            for o in self.ops[e]:
                if o.key is None and o.needs_inc:
                    if cnt >= SEM_LIMIT:
                        gen += 1
                        cnt = 0
                        esems[e].append(es.enter_context(nc.semaphore("s_%s_%d" % (e, gen))))
                    cnt += 1
                    o.gen, o.count = gen, cnt
        ksems = {k: es.enter_context(nc.semaphore("k_%s" % k)) for k in self.keycount}
        block = es.enter_context(nc.Block())
        handles = {"pe": block.tensor, "act": block.scalar, "dve": block.vector, "pool": block.gpsimd, "sp": block.sync}

        def make(e):
            def body(eh):
                for o in self.ops[e]:
                    ws = [(ksems[p.key], 16 * p.kidx) if p.key is not None else (esems[p.eng][p.gen], p.count)
                          for p in o.deps]
                    inline = ws.pop() if (ws and INLINE_WAIT) else None
                    for sem, val in ws:
                        eh.wait_ge(sem, val)
                    ins = o.fn(eh)
                    if inline is not None:
                        ins = ins._wait_ge(inline[0], inline[1])
                    if o.key is not None:
                        ins.then_inc(ksems[o.key], 16)
                    elif o.needs_inc:
                        ins.then_inc(esems[e][o.gen], 1)
                if e == "pool":
                    for k, n in final_waits:
                        eh.wait_ge(ksems[k], 16 * n)
            return body

        for e in self.ENGS:
            handles[e](make(e))


class Cfg:
    def __init__(self, S, past):
        self.S = S
        self.past = past
        self.NT1 = S // 512
        self.NT2 = S // 1024
        self.NMY = S // 256
        self.NPC = past // 128


def build_program(cfg):
    VC = 80
    nc = bass.Bass("TRN2", target_bir_lowering=False)
    S, NT1, NT2, NMY = cfg.S, cfg.NT1, cfg.NT2, cfg.NMY
    MYTOK = NMY * 128
    NKP = S // 128
    NKS = cfg.NPC + 1

    def din(name, shape):
        return nc.dram_tensor(name, list(shape), F32, kind="ExternalInput").ap()

    def dout(name, shape):
        return nc.dram_tensor(name, list(shape), F32, kind="ExternalOutput").ap()

    def dscr(name, shape, dt=BF16):
        return nc.dram_tensor(name, list(shape), dt, kind="Internal").ap()

    x_p = din("x_p", [128, 8, S])
    x_s = din("x_s", [128, 8, 64])
    c_kv = din("c_kv", [cfg.past, 160])
    w_gu = [[din("w_gu%d_%d" % (f, l), [11, 128, 4096]) for l in range(DEPTH)] for f in range(2)]
    w_dn = [[din("w_dn%d_%d" % (f, l), [4, 128, 5632]) for l in range(DEPTH)] for f in range(2)]
    w_inu = [din("w_inu%d" % l, [2, 128, 4096]) for l in range(N_A)]
    w_inv = [din("w_inv%d" % l, [2, 128, 4096]) for l in range(N_A)]
    w_out = [din("w_out%d" % l, [2, 128, 4096]) for l in range(N_A)]
    w_dkv = din("w_dkv", [1, 128, 1280])
    w_dq = [din("w_dq%d" % j, [1, 128, 2048]) for j in range(2)]
    w_uq = [din("w_uq%d" % j, [1, 128, 3072]) for j in range(2)]
    w_uqr = [din("w_uqr%d" % j, [1, 128, 3072]) for j in range(2)]
    w_o = [din("w_o%d" % j, [4, 64, 4096]) for j in range(2)]
    w_uk = din("w_uk", [128, 1024])
    w_uv = din("w_uv", [128, 1024])
    w_sT = din("w_sT", [N_A, 128, 1024])
    b_s = din("b_s", [N_A, 1, 1024])
    gains = din("gains", [128, 14 * 8])
    gq = din("gq", [128, 4])
    vg = din("vg", [N_A, 1, 1024])
    ckvg = din("ckvg", [1, 128])
    ident_d = din("ident", [128, 128])
    isel_d = din("isel", [32, 96])
    mdiag_d = din("mdiag", [128, 128])
    modd_d = din("modd", [128, 128])
    mut_d = din("mut", [128, 128])
    rope_tok_p = din("rope_tok_p", [S, 48])
    rope_tok_s = din("rope_tok_s", [64, 48])
    rope_fm_p = din("rope_fm_p", [32, 2, MYTOK])
    rope_fm_s = din("rope_fm_s", [32, 2, 64])

    y_p = dout("y_p", [128, 8, MYTOK])
    y_s = dout("y_s", [128, 8, 64])
    o_kv_p = dout("o_kv_p", [MYTOK, 160])
    o_kv_s = dout("o_kv_s", [64, 160])
    o_av_s = dout("o_av_s", [N_A, 64, D])

    s_gu = [[dscr("s_gu%d_%d" % (f, l), [11, 128, 4096]) for l in range(DEPTH)] for f in range(2)]
    s_dn = [[dscr("s_dn%d_%d" % (f, l), [4, 128, 5632]) for l in range(DEPTH)] for f in range(2)]
    s_inu = [dscr("s_inu%d" % l, [2, 128, 4096]) for l in range(N_A)]
    s_inv = [dscr("s_inv%d" % l, [2, 128, 4096]) for l in range(N_A)]
    s_out = [dscr("s_out%d" % l, [2, 128, 4096]) for l in range(N_A)]
    s_dkv = dscr("s_dkv", [1, 128, 1280])
    s_dq = [dscr("s_dq%d" % j, [1, 128, 2048]) for j in range(2)]
    s_uq = [dscr("s_uq%d" % j, [1, 128, 3072]) for j in range(2)]
    s_uqr = [dscr("s_uqr%d" % j, [1, 128, 3072]) for j in range(2)]
    s_o = [dscr("s_o%d" % j, [4, 64, 4096]) for j in range(2)]
    kt_p = dscr("kt_p", [NH, 96, NKP * 128])
    v_p = dscr("v_p", [128, NH, NKP, VC])
    kt_s = dscr("kt_s", [NH, 96, NKS * 128])
    v_s = dscr("v_s", [128, NH, NKS, VC])
    hmid_p = dscr("hmid_p", [max(NT2, 1), 128, 8, 512], F32)
    hmid_s = dscr("hmid_s", [128, 8, 64], F32)

    es = ExitStack()
    sch = Sched(nc)
    bufs = {}

    def B(name):
        if name not in bufs:
            bufs[name] = Buf(name)
        return bufs[name]

    def sb(name, shape, dt=F32):
        return es.enter_context(nc.sbuf_tensor(name, list(shape), dt))

    hT = sb("hT", [128, 8, 512])
    xn = sb("xn", [128, 8, 512], BF16)
    sq = sb("sq", [128, 8, 512], BF16)
    sd = sb("sd", [128, 512])
    rstd = sb("rstd", [128, 512])
    actb = sb("actb", [128, NJ, 512], BF16)
    sg = [sb("sg%d" % i, [128, 512]) for i in range(2)]
    bt = [sb("bt%d" % i, [128, 512]) for i in range(2)]
    U32 = sb("U32", [128, 8192])
    U16 = sb("U16", [128, 18048], BF16)

    def carve(U, off, n, pat=None, np_=128, **kw):
        v = U[0:np_, off:off + n]
        return v.rearrange(pat, **kw) if pat else v

    uT = carve(U32, 0, 4096, "p (c t) -> p c t", c=8)
    vtok = carve(U32, 4096, 1024)
    vnf = carve(U32, 5120, 1024)
    vtok2 = [vtok, carve(U32, 7168, 1024)]
    tmpg = [carve(U32, 6144 + i * 512, 512) for i in range(2)]
    gated = carve(U16, 0, 4096, "p (c t) -> p c t", c=8)
    vnb = [carve(U16, 4096 + i * 1024, 1024) for i in range(2)] + [carve(U16, 15360 + i * 1024, 1024) for i in range(2)]
    ktsb = carve(U16, 6144, 4096, "p (c t) -> p c t", np_=96, c=8)
    vexp = carve(U16, 10240, NH * 4 * VC, "p (h c d) -> p h c d", h=NH, c=4)
    yfm = carve(U32, 0, 4096, "p (c t) -> p c t", c=8)
    qlat = carve(U32, 4096, 1024, "p (c t) -> p c t", c=2)
    ropef = carve(U32, 5120, 1024, "p (c t) -> p c t", np_=96, c=2)
    t1 = carve(U32, 6144, 512, np_=96)
    t2 = carve(U32, 6656, 512, np_=96)
    rden = carve(U32, 7168, 512, np_=96)
    bcsb = carve(U32, 7680, 512, np_=64)
    qn = carve(U16, 0, 1024, "p (c t) -> p c t", c=2)
    pT = [carve(U16, 1024 + i * 512, 512) for i in range(3)] + [carve(U16, 17504, 512)]
    kblk = [carve(U16, 2560 + i * 2048, 2048, np_=96) for i in range(2)]
    vblk = [carve(U16, 6656 + i * 1328, KVB * VC, "p (k d) -> p k d", k=KVB) for i in range(2)]
    vflat = [carve(U16, 6656 + i * 1328, 1328) for i in range(2)]
    oT = carve(U16, 9312, 8192, "p (h t) -> p h t", np_=64, h=NH)
    small = sb("small", [128, 16])
    kvsb2 = [sb("kvsb%d" % i, [128, 160]) for i in range(2)]
    lat2 = [sb("lat%d" % i, [128, 160]) for i in range(2)]
    rt2 = [sb("rt%d" % i, [128, 64]) for i in range(2)]
    kvsb = kvsb2[0]
    ropet = sb("ropet", [128, 4, 48])
    ckvT = sb("ckvT", [128, 512], BF16)
    krT = sb("krT", [32, 512], BF16)
    ring = [sb("ring%d" % i, [128, SLOT], BF16) for i in range(NSLOT)]
    wsT = sb("wsT", [128, N_A, 8, 128], BF16)
    bias_bc = sb("bias_bc", [128, N_A, 1024])
    vg_bc = sb("vg_bc", [128, N_A, 1024])
    ckvg_bc = sb("ckvg_bc", [128, 128])
    wuk_ext = sb("wuk_ext", [128, NH, 96], BF16)
    wuv_b = sb("wuv_b", [128, 1024], BF16)
    ident = sb("ident_s", [128, 128])
    isel = sb("isel_s", [32, 96], BF16)
    ones_m = sb("ones_m", [128, 128], BF16)
    ones_q = sb("ones_q", [128, 128], BF16)
    ones_f = sb("ones_f", [128, 64])
    eps_t = sb("eps_t", [128, 1])
    gains_s = sb("gains_s", [128, 14 * 8])
    gq_s = sb("gq_s", [128, 4])
    mdiag = sb("mdiag_s", [128, 128], BF16)
    modd = sb("modd_s", [128, 128], BF16)
    mut = sb("mut_s", [128, 128])
    qT = actb
    stage = vtok

    ppool = {}
    for pool, n in (("mm", 4), ("acc", 2), ("misc", 2)):
        ppool[pool] = [es.enter_context(nc.psum_tensor("ps_%s%d" % (pool, i), [128, 512], F32)) for i in range(n)]
    pctr = {"mm": 0, "acc": 0, "misc": 0}

    def psum(pool):
        i = pctr[pool] % len(ppool[pool])
        pctr[pool] += 1
        nm = "ps_%s%d" % (pool, i)
        if nm not in bufs:
            bufs[nm] = Buf(nm, excl=True)
        return ppool[pool][i], bufs[nm]

    def mm(out, lhsT, rhs, start, stop, reads, writes):
        sch.op("pe", lambda e: e.matmul(out, lhsT=lhsT, rhs=rhs, start=start, stop=stop), reads=reads, writes=writes)

    def tr(out, in_, idn, reads, writes):
        sch.op("pe", lambda e: e.transpose(out, in_, idn), reads=reads, writes=writes)

    def act(out, in_, func, reads, writes, **kw):
        sch.op("act", lambda e: e.activation(out=out, in_=in_, func=func, **kw), reads=reads, writes=writes)

    def cp(eng, out, in_, reads, writes):
        if eng == "act":
            act(out, in_, AF.Copy, reads, writes)
        else:
            sch.op(eng, lambda e: e.tensor_copy(out=out, in_=in_), reads=reads, writes=writes)

    def tt(eng, out, in0, in1, op, reads, writes):
        sch.op(eng, lambda e: e.tensor_tensor(out=out, in0=in0, in1=in1, op=op), reads=reads, writes=writes)

    def stt(eng, out, in0, scalar, in1, op0, op1, reads, writes):
        sch.op(eng, lambda e: e.scalar_tensor_tensor(out=out, in0=in0, scalar=scalar, in1=in1, op0=op0, op1=op1),
               reads=reads, writes=writes)

    def recip(out, in_, reads, writes):
        sch.op("dve", lambda e: e.reciprocal(out=out, in_=in_), reads=reads, writes=writes)

    def dma(eng, out, in_, reads, writes, key):
        sch.op(eng, lambda e: e.dma_start(out=out, in_=in_), reads=reads, writes=writes, key=key)

    alt = {"n": 0}

    def evac_eng():
        alt["n"] += 1
        return "act" if alt["n"] % 2 else "dve"

    rctr = {"n": 0}

    def wfetch(src_ap, np_, n, scratch_buf):
        while cast_pending.get(id(scratch_buf), 0) > 0:
            pump_casts(1)
        i = rctr["n"] % NSLOT
        rctr["n"] += 1
        dma("sp", ring[i][0:np_, 0:n], src_ap, [scratch_buf], [B("ring%d" % i)], "ring%d" % i)
        return ring[i], B("ring%d" % i)

    sch.op("dve", lambda e: e.memset(ones_m[:], 1.0 / 1024.0), writes=[B("ones_m")])
    sch.op("dve", lambda e: e.memset(ones_q[:], 1.0 / 256.0), writes=[B("ones_q")])
    sch.op("dve", lambda e: e.memset(ones_f[:], 1.0), writes=[B("ones_f")])
    sch.op("dve", lambda e: e.memset(eps_t[:], EPS), writes=[B("eps_t")])
    sch.op("dve", lambda e: e.memset(vexp[:], 1.0), writes=[B("vexp")])
    sch.op("dve", lambda e: e.memset(wuk_ext[:], 0.0), writes=[B("wuk_ext")])

    def load_const(dst, src, name):
        dma("sp", dst, src, [], [B(name)], "c_" + name)

    load_const(gains_s[:], gains[:, :], "gains")
    load_const(gq_s[:], gq[:, :], "gq")
    load_const(ident[:], ident_d[:, :], "ident")
    load_const(mut[:], mut_d[:, :], "mut")
    for l in range(N_A):
        load_const(bias_bc[:, l, :], b_s[l, :, :].partition_broadcast(128), "bias_bc")
        load_const(vg_bc[:, l, :], vg[l, :, :].partition_broadcast(128), "vg_bc")
    load_const(ckvg_bc[:], ckvg[:, :].partition_broadcast(128), "ckvg_bc")

    def staged(src_ap, np_, n, fn):
        dma("sp", stage[0:np_, 0:n], src_ap, [], [B("vtok0")], "c_stage")
        fn()

    staged(isel_d[:, :], 32, 96, lambda: cp("dve", isel[:], stage[0:32, 0:96], [B("vtok0")], [B("isel")]))
    staged(mdiag_d[:, :], 128, 128, lambda: cp("dve", mdiag[:], stage[:, 0:128], [B("vtok0")], [B("mdiag")]))
    staged(modd_d[:, :], 128, 128, lambda: cp("dve", modd[:], stage[:, 0:128], [B("vtok0")], [B("modd")]))
    staged(w_uv[:, :], 128, 1024, lambda: cp("dve", wuv_b[:], stage[:, :], [B("vtok0")], [B("wuv_b")]))
    staged(w_uk[:, :], 128, 1024,
           lambda: cp("dve", wuk_ext[:, :, 0:64], stage[:, :].rearrange("p (h d) -> p h d", h=NH),
                      [B("vtok0")], [B("wuk_ext")]))
    for l in range(N_A):
        def f(l=l):
            for g in range(8):
                tt("dve", wsT[:, l, g, :], stage[:, g * 128:(g + 1) * 128], mut[:], ALU.mult,
                   [B("vtok0"), B("mut")], [B("wsT")])
        staged(w_sT[l], 128, 1024, f)

    cast_gate = [None]

    cast_queue = []
    cast_lazy = [False]
    cast_pending = {}

    def cast(dst, src, name, pieces):
        b = B("scr_" + name)
        for i in range(pieces):
            def f(i=i):
                dma("pool", dst[i], src[i], [B("actb")], [b], "cast_" + name)
                cast_pending[id(b)] -= 1
            if cast_lazy[0]:
                cast_pending[id(b)] = cast_pending.get(id(b), 0) + 1
                cast_queue.append(f)
            else:
                dma("pool", dst[i], src[i], [], [b], "cast_" + name)
        return b

    def pump_casts(n):
        for _ in range(n):
            if cast_queue:
                cast_queue.pop(0)()

    scr = {}

    def emit_casts(l):
        scr[("gu", 0, l)] = cast(s_gu[0][l], w_gu[0][l], "gu0_%d" % l, 11)
        cast_lazy[0] = True
        scr[("dn", 0, l)] = cast(s_dn[0][l], w_dn[0][l], "dn0_%d" % l, 4)
        if l < N_A:
            scr[("inu", l)] = cast(s_inu[l], w_inu[l], "inu%d" % l, 2)
            scr[("inv", l)] = cast(s_inv[l], w_inv[l], "inv%d" % l, 2)
            scr[("out", l)] = cast(s_out[l], w_out[l], "out%d" % l, 2)
        else:
            j = l - N_A
            scr[("dq", j)] = cast(s_dq[j], w_dq[j], "dq%d" % j, 1)
            scr[("uq", j)] = cast(s_uq[j], w_uq[j], "uq%d" % j, 1)
            scr[("uqr", j)] = cast(s_uqr[j], w_uqr[j], "uqr%d" % j, 1)
            scr[("o", j)] = cast(s_o[j], w_o[j], "o%d" % j, 4)
        scr[("gu", 1, l)] = cast(s_gu[1][l], w_gu[1][l], "gu1_%d" % l, 11)
        scr[("dn", 1, l)] = cast(s_dn[1][l], w_dn[1][l], "dn1_%d" % l, 4)
        if l == N_A - 1:
            scr[("dkv",)] = cast(s_dkv, w_dkv, "dkv", 1)

    emit_casts(0)
    cast_lazy[0] = True
    for l_ in range(1, DEPTH):
        emit_casts(l_)
    pump_rate = [8]

    HT = [B("hT%d" % kc) for kc in range(8)]

    def rmsnorm_fm(T, gcol, nchunks=8, src=None, dst=None, ones=None, gtile=None, dst_is_y=False):
        src = hT if src is None else src
        dst = xn if dst is None else dst
        if dst_is_y:
            dst = yfm
        ones = ones_m if ones is None else ones
        gtile = gains_s if gtile is None else gtile
        sbufs = HT if src is hT else [B("qlat")] * 8
        dname = "xn" if dst is xn else ("qn" if dst is qn else "yfm")
        p, pb = psum("misc")
        for kc in range(nchunks):
            if kc % 2 == 0:
                act(sq[:, kc, 0:T], src[:, kc, 0:T], AF.Square, [sbufs[kc]], [B("sq%d" % kc)])
            else:
                tt("pool", sq[:, kc, 0:T], src[:, kc, 0:T], src[:, kc, 0:T], ALU.mult, [sbufs[kc]], [B("sq%d" % kc)])
            mm(p[:, 0:T], ones[:, :], sq[:, kc, 0:T], kc == 0, kc == nchunks - 1, [B("sq%d" % kc), B("ones_m"), B("ones_q")], [pb])
        act(sd[:, 0:T], p[:, 0:T], AF.Sqrt, [pb, B("eps_t")], [B("sd")], bias=eps_t[:, 0:1], scale=1.0)
        recip(rstd[:, 0:T], sd[:, 0:T], [B("sd")], [B("rstd")])
        for kc in range(nchunks):
            stt("dve", dst[:, kc, 0:T], src[:, kc, 0:T], gtile[:, gcol + kc:gcol + kc + 1], rstd[:, 0:T], ALU.mult, ALU.mult,
                [sbufs[kc], B("rstd"), B("gains"), B("gq")], [B("%s%d" % (dname, kc))])

    XN = [B("xn%d" % kc) for kc in range(8)]

    pending_expand = []

    def norm_deferred(T, gcol):
        p, pb = psum("misc")
        for kc in range(8):
            if kc % 2 == 0:
                act(sq[:, kc, 0:T], hT[:, kc, 0:T], AF.Square, [HT[kc]], [B("sq%d" % kc)])
            else:
                tt("pool", sq[:, kc, 0:T], hT[:, kc, 0:T], hT[:, kc, 0:T], ALU.mult, [HT[kc]], [B("sq%d" % kc)])
            g_ = gains_s[:, gcol + kc:gcol + kc + 1]
            if kc % 2 == 1:
                act(xn[:, kc, 0:T], hT[:, kc, 0:T], AF.Copy, [HT[kc], B("gains")], [XN[kc]], scale=g_)
            else:
                sch.op("dve", lambda e, kc=kc, g_=g_: e.tensor_scalar_mul(xn[:, kc, 0:T], hT[:, kc, 0:T], g_),
                       reads=[HT[kc], B("gains")], writes=[XN[kc]])
            mm(p[:, 0:T], ones_m[:, :], sq[:, kc, 0:T], kc == 0, kc == 7, [B("sq%d" % kc), B("ones_m")], [pb])
        act(sd[:, 0:T], p[:, 0:T], AF.Sqrt, [pb, B("eps_t")], [B("sd")], bias=eps_t[:, 0:1], scale=1.0)
        recip(rstd[:, 0:T], sd[:, 0:T], [B("sd")], [B("rstd")])

    def ffn(T, f, l):
        norm_deferred(T, ((0 if f == 0 else 8) + l) * 8)
        sb_gu = scr[("gu", f, l)]
        pump_casts(pump_rate[0])
        for jb in range(11):
            w, wb = wfetch(s_gu[f][l][jb], 128, 4096, sb_gu)
            wv = w[:, 0:4096].rearrange("p (jj kc gu m) -> p jj kc gu m", jj=2, kc=8, gu=2)
            banks = [(psum("mm"), psum("mm")) for jj in range(2)]
            if jb == 0:
                for kc in range(8):
                    for jj in range(2):
                        (pg, pgb), (pu, pub) = banks[jj]
                        mm(pg[:, 0:T], wv[:, jj, kc, 0, :], xn[:, kc, 0:T], kc == 0, kc == 7, [wb, XN[kc]], [pgb])
                        mm(pu[:, 0:T], wv[:, jj, kc, 1, :], xn[:, kc, 0:T], kc == 0, kc == 7, [wb, XN[kc]], [pub])
            for jj in range(2):
                j = 2 * jb + jj
                (pg, pgb), (pu, pub) = banks[jj]
                if jb != 0:
                    for kc in range(8):
                        mm(pg[:, 0:T], wv[:, jj, kc, 0, :], xn[:, kc, 0:T], kc == 0, kc == 7, [wb, XN[kc]], [pgb])
                    for kc in range(8):
                        mm(pu[:, 0:T], wv[:, jj, kc, 1, :], xn[:, kc, 0:T], kc == 0, kc == 7, [wb, XN[kc]], [pub])
                s_, sgb = sg[j % 2], B("sg%d" % (j % 2))
                b_, btb = bt[j % 2], B("bt%d" % (j % 2))
                tt("dve", s_[:, 0:T], pg[:, 0:T], rstd[:, 0:T], ALU.mult, [pgb, B("rstd")], [sgb])
                act(s_[:, 0:T], s_[:, 0:T], AF.Silu, [], [sgb])
                tt("dve", b_[:, 0:T], pu[:, 0:T], rstd[:, 0:T], ALU.mult, [pub, B("rstd")], [btb])
                tt("pool", actb[:, j, 0:T], b_[:, 0:T], s_[:, 0:T], ALU.mult, [btb, sgb], [B("actb")])
        while pending_expand:
            pending_expand.pop(0)()
        sb_dn = scr[("dn", f, l)]
        pump_casts(pump_rate[0])
        for ob in range(4):
            w, wb = wfetch(s_dn[f][l][ob], 128, 5632, sb_dn)
            wv = w[:, 0:5632].rearrange("p (oo j m) -> p oo j m", oo=2, j=NJ)
            for oo in range(2):
                oc = 2 * ob + oo
                py, pyb = psum("acc")
                for j in range(NJ):
                    mm(py[:, 0:T], wv[:, oo, j, :], actb[:, j, 0:T], j == 0, j == NJ - 1, [wb, B("actb")], [pyb])
                stt("dve", hT[:, oc, 0:T], py[:, 0:T], 0.5, hT[:, oc, 0:T], ALU.mult, ALU.add, [pyb], [HT[oc]])

    def chunks_of(T):
        return [(c * 128, min(128, T - c * 128)) for c in range((T + 127) // 128)]

    def gmlp(T, l, av_out):
        rmsnorm_fm(T, (4 + l) * 8)
        chs = chunks_of(T)
        wvs = []
        for nb in range(2):
            w, wb = wfetch(s_inv[l][nb], 128, 4096, scr[("inv", l)])
            wvs.append((w[:, 0:4096].rearrange("p (kc n) -> p kc n", kc=8), wb))
        for ci, (c0, tc) in enumerate(chs):
            vt_, vtb_ = vtok2[ci % 2], B("vtok%d" % (ci % 2))
            pbs = [psum("mm") for nb in range(2)]
            if ci == 0:
                for kc in range(8):
                    for nb in range(2):
                        mm(pbs[nb][0][0:tc, :], xn[:, kc, c0:c0 + tc], wvs[nb][0][:, kc, :], kc == 0, kc == 7, [wvs[nb][1], XN[kc]], [pbs[nb][1]])
            for nb in range(2):
                p, pb = pbs[nb]
                wv, wb = wvs[nb]
                if ci != 0:
                    for kc in range(8):
                        mm(p[0:tc, :], xn[:, kc, c0:c0 + tc], wv[:, kc, :], kc == 0, kc == 7, [wb, XN[kc]], [pb])
                act(vt_[0:tc, nb * 512:(nb + 1) * 512], p[0:tc, :], AF.Gelu, [pb], [vtb_])
            sm = 8 * (ci % 2)
            smb = B("small%d" % (ci % 2))
            act(vnf[0:tc, :], vt_[0:tc, :], AF.Square, [vtb_], [B("vnf"), smb], accum_out=small[0:tc, sm:sm + 1])
            act(small[0:tc, sm + 1:sm + 2], small[0:tc, sm:sm + 1], AF.Sqrt, [smb, B("eps_t")], [smb], bias=eps_t[0:tc, 0:1], scale=1.0 / 1024.0)
            recip(small[0:tc, sm + 2:sm + 3], small[0:tc, sm + 1:sm + 2], [smb], [smb])
            vb = vnb[ci % 4]
            vbb = B("vnb%d" % (ci % 4))
            if av_out is not None:
                stt("dve", vnf[0:tc, :], vt_[0:tc, :], small[0:tc, sm + 2:sm + 3], vg_bc[0:tc, l, :], ALU.mult, ALU.mult,
                    [vtb_, smb, B("vg_bc")], [B("vnf")])
                dma("pool", av_out[l, c0:c0 + tc, :], vnf[0:tc, :], [B("vnf")], [B("o_av")], "st_av")
                cp("dve", vb[0:tc, :], vnf[0:tc, :], [B("vnf")], [vbb])
            else:
                stt("dve", vb[0:tc, :], vt_[0:tc, :], small[0:tc, sm + 2:sm + 3], vg_bc[0:tc, l, :], ALU.mult, ALU.mult,
                    [vtb_, smb, B("vg_bc")], [vbb])
        for ob in range(2):
            w, wb = wfetch(s_inu[l][ob], 128, 4096, scr[("inu", l)])
            wv = w[:, 0:4096].rearrange("p (oo kc m) -> p oo kc m", oo=4, kc=8)
            for oo in range(4):
                oc = 4 * ob + oo
                p, pb = psum("mm")
                for kc in range(8):
                    mm(p[:, 0:T], wv[:, oo, kc, :], xn[:, kc, 0:T], kc == 0, kc == 7, [wb, XN[kc]], [pb])
                act(uT[:, oc, 0:T], p[:, 0:T], AF.Gelu, [pb], [B("uT%d" % oc)])
        for ci, (c0, tc) in enumerate(chs):
            vb = vnb[ci % 4]
            vbb = B("vnb%d" % (ci % 4))
            for half in range(2):
                p, pb = psum("mm")
                for gi in range(4):
                    g = half * 4 + gi
                    mm(p[:, gi * 128:gi * 128 + tc], vb[0:tc, g * 128:(g + 1) * 128], wsT[0:tc, l, g, 0:tc], True, True,
                       [vbb, B("wsT")], [pb])
                tg = tmpg[half]
                tgb = B("tmpg%d" % half)
                pv = p[:, :].rearrange("p (g t) -> p g t", g=4)[:, :, 0:tc]
                tv = tg[:, :].rearrange("p (g t) -> p g t", g=4)[:, :, 0:tc]
                bv = bias_bc[:, l, half * 512:(half + 1) * 512].rearrange("p (g t) -> p g t", g=4)[:, :, 0:tc]
                tt("dve", tv, pv, bv, ALU.add, [pb, B("bias_bc")], [tgb])
                tt("dve" if half == 0 else "pool", gated[:, half * 4:half * 4 + 4, c0:c0 + tc], tv, uT[:, half * 4:half * 4 + 4, c0:c0 + tc], ALU.mult,
                   [tgb] + [B("uT%d" % (half * 4 + k)) for k in range(4)], [B("gated%d" % (half * 4 + k)) for k in range(4)])
        for ob in range(2):
            w, wb = wfetch(s_out[l][ob], 128, 4096, scr[("out", l)])
            wv = w[:, 0:4096].rearrange("p (oo kc m) -> p oo kc m", oo=4, kc=8)
            for oo in range(4):
                oc = 4 * ob + oo
                p, pb = psum("acc")
                for g in range(8):
                    mm(p[:, 0:T], wv[:, oo, g, :], gated[:, g, 0:T], g == 0, g == 7, [wb, B("gated%d" % g)], [pb])
                tt("dve", hT[:, oc, 0:T], p[:, 0:T], hT[:, oc, 0:T], ALU.add, [pb], [HT[oc]])

    def to_kvT(src, c0, tc, srcbuf):
        import os
        dbg = int(os.environ.get("KDBG", "15"))
        p, pb = psum("misc")
        if dbg & 1:
            tr(p[:, 0:tc], src[0:tc, 0:128], ident[0:tc, 0:tc], [srcbuf, B("ident")], [pb])
        if dbg & 2:
            tr(p[0:32, 128:128 + tc], src[0:tc, 128:160], ident[0:tc, 0:tc], [srcbuf, B("ident")], [pb])
        if dbg & 4:
            cp("act", ckvT[:, c0:c0 + tc], p[:, 0:tc], [pb], [B("ckvT")])
        if dbg & 8:
            cp("dve", krT[:, c0:c0 + tc], p[0:32, 128:128 + tc], [pb], [B("krT")])

    def expand(T, kt_scr, v_scr, kb0):
        for hg in range(2):
            for hh in range(8):
                h = hg * 8 + hh
                p, pb = psum("mm")
                mm(p[0:96, 0:T], wuk_ext[:, h, :], ckvT[:, 0:T], True, False, [B("wuk_ext"), B("ckvT")], [pb])
                mm(p[0:96, 0:T], isel[:, :], krT[:, 0:T], False, True, [B("isel"), B("krT")], [pb])
                cp(evac_eng(), ktsb[:, hh, 0:T], p[0:96, 0:T], [pb], [B("ktsb")])
            dst = kt_scr[hg * 8:hg * 8 + 8, :, kb0 * 128:kb0 * 128 + T].rearrange("h p t -> p h t")
            dma("pool", dst, ktsb[:, :, 0:T], [B("ktsb")], [B("kt_scr")], "st_kt")
        chs = chunks_of(T)
        for ci, (c0, tc) in enumerate(chs):
            for nb in range(2):
                p, pb = psum("mm")
                mm(p[0:tc, :], ckvT[:, c0:c0 + tc], wuv_b[:, nb * 512:(nb + 1) * 512], True, True, [B("ckvT"), B("wuv_b")], [pb])
                cp(evac_eng(), vexp[0:tc, nb * 8:(nb + 1) * 8, ci, 0:64], p[0:tc, :].rearrange("p (h d) -> p h d", h=8), [pb], [B("vexp")])
        tcl = chs[-1][1]
        if tcl == 128:
            dma("pool", v_scr[:, :, kb0:kb0 + len(chs), :], vexp[:, :, 0:len(chs), :], [B("vexp")], [B("v_scr")], "st_v")
        else:
            dma("pool", v_scr[0:tcl, :, kb0:kb0 + 1, :], vexp[0:tcl, :, 0:1, :], [B("vexp")], [B("v_scr")], "st_v")

    def latent(T, rope_src, kv_out_fn, kt_scr, v_scr, kb0):
        rmsnorm_fm(T, 12 * 8)
        w, wb = wfetch(s_dkv[0], 128, 1280, scr[("dkv",)])
        wv = w[:, 0:1280].rearrange("p (kc n) -> p kc n", kc=8)
        chs = chunks_of(T)
        dma("sp", ropet[0:min(T, 128), 0:len(chs), :], rope_src, [], [B("ropet")], "ld_ropet")
        for ci, (c0, tc) in enumerate(chs):
            kv_, kvb = kvsb2[ci % 2], B("kvsb%d" % (ci % 2))
            la_, lab = lat2[ci % 2], B("lat%d" % (ci % 2))
            r_, rb = rt2[ci % 2], B("rt%d" % (ci % 2))
            sm = 8 * (ci % 2)
            smb = B("small%d" % (ci % 2))
            p, pb = psum("misc")
            for kc in range(8):
                mm(p[0:tc, 0:160], xn[:, kc, c0:c0 + tc], wv[:, kc, :], kc == 0, kc == 7, [wb, XN[kc]], [pb])
            cp("act", kv_[0:tc, :], p[0:tc, 0:160], [pb], [kvb])
            act(la_[0:tc, 0:128], kv_[0:tc, 0:128], AF.Square, [kvb], [lab, smb], accum_out=small[0:tc, sm + 4:sm + 5])
            act(small[0:tc, sm + 5:sm + 6], small[0:tc, sm + 4:sm + 5], AF.Sqrt, [smb, B("eps_t")], [smb], bias=eps_t[0:tc, 0:1], scale=1.0 / 128.0)
            recip(small[0:tc, sm + 6:sm + 7], small[0:tc, sm + 5:sm + 6], [smb], [smb])
            stt("dve", la_[0:tc, 0:128], kv_[0:tc, 0:128], small[0:tc, sm + 6:sm + 7], ckvg_bc[0:tc, :], ALU.mult, ALU.mult,
                [kvb, smb, B("ckvg_bc")], [lab])
            tt("dve", r_[0:tc, 0:32], kv_[0:tc, 128:160], ropet[0:tc, ci, 0:32], ALU.mult, [kvb, B("ropet")], [rb])
            tt("pool", r_[0:tc, 32:48], kv_[0:tc, 144:160], ropet[0:tc, ci, 32:48], ALU.mult, [kvb, B("ropet")], [rb])
            tt("pool", r_[0:tc, 48:64], kv_[0:tc, 128:144], ropet[0:tc, ci, 32:48], ALU.mult, [kvb, B("ropet")], [rb])
            tt("dve", la_[0:tc, 128:144], r_[0:tc, 0:16], r_[0:tc, 32:48], ALU.subtract, [rb], [lab])
            tt("dve", la_[0:tc, 144:160], r_[0:tc, 16:32], r_[0:tc, 48:64], ALU.add, [rb], [lab])
            dst = kv_out_fn(ci, tc)
            if dst is not None:
                dma("pool", dst, la_[0:tc, :], [lab], [B("o_kv")], "st_kv%d" % (ci % 2))
            to_kvT(la_, c0, tc, lab)
        pending_expand.append(lambda: expand(T, kt_scr, v_scr, kb0))

    def mla(T, j, rope_src, kt_scr, v_scr, segs):
        l = N_A + j
        norm_deferred(T, (4 + l) * 8)
        dma("sp", ropef[64:96, :, 0:T], rope_src, [], [B("ropef")], "ld_ropef")
        w, wb = wfetch(s_dq[j][0], 128, 2048, scr[("dq", j)])
        wv = w[:, 0:2048].rearrange("p (oc kc m) -> p oc kc m", oc=2, kc=8)
        for oc in range(2):
            p, pb = psum("mm")
            for kc in range(8):
                mm(p[:, 0:T], wv[:, oc, kc, :], xn[:, kc, 0:T], kc == 0, kc == 7, [wb, XN[kc]], [pb])
            tt("dve", qlat[:, oc, 0:T], p[:, 0:T], rstd[:, 0:T], ALU.mult, [pb, B("rstd")], [B("qlat")])
        rmsnorm_fm(T, j * 2, nchunks=2, src=qlat, dst=qn, ones=ones_q, gtile=gq_s)
        QN = [B("qn0"), B("qn1")]
        w1, wb1 = wfetch(s_uq[j][0], 128, 3072, scr[("uq", j)])
        w2, wb2 = wfetch(s_uqr[j][0], 128, 3072, scr[("uqr", j)])
        wv1 = w1[:, 0:3072].rearrange("p (kc h m) -> p kc h m", kc=2, h=NH)
        wv2 = w2[:, 0:3072].rearrange("p (kc h m) -> p kc h m", kc=2, h=NH)
        QT = [B("qT%d" % h) for h in range(NH)]

        def qproj(h):
            pq, pqb = psum("misc")
            pr, prb = psum("misc")
            for kc in range(2):
                mm(pq[0:128, 0:T], w1[:, kc * 1536 + h * 96:kc * 1536 + h * 96 + 128], qn[:, kc, 0:T], kc == 0, kc == 1, [wb1, QN[kc]], [pqb])
            for kc in range(2):
                mm(pr[0:128, 0:T], w2[:, kc * 1536 + h * 96:kc * 1536 + h * 96 + 128], qn[:, kc, 0:T], kc == 0, kc == 1, [wb2, QN[kc]], [prb])
            cp("act", qT[0:64, h, 0:T], pq[0:64, 0:T], [pqb], [QT[h]] + ([B("actb")] if h == 0 else []))
            tt("dve", t1[64:96, 0:T], pq[64:96, 0:T], ropef[64:96, 0, 0:T], ALU.mult, [pqb, B("ropef")], [B("t1")])
            tt("dve", t2[64:96, 0:T], pr[64:96, 0:T], ropef[64:96, 1, 0:T], ALU.mult, [prb, B("ropef")], [B("t2")])
            tt("pool", qT[64:96, h, 0:T], t1[64:96, 0:T], t2[64:96, 0:T], ALU.add, [B("t1"), B("t2")], [QT[h]])

        qproj(0)
        nblk = (segs[-1][0] // KVB) + 1
        blk_order = [nblk - 1] + list(range(nblk - 1))
        ordered = [s_ for blk in blk_order for s_ in segs if s_[0] // KVB == blk]
        assert ordered[0][2] == 0
        kvctr = [0]
        pctr_ = [0]
        epi_pending = []
        epia_pending = []
        for h in range(NH):
            po, pob = psum("acc")
            pend = []
            nseg_done = [0]

            def pv(item):
                (kbl, nk, q0, vt, vtb, pt, ptb, first, last) = item
                mm(po[0:128, q0:T], vt[0:nk, kbl * VC:kbl * VC + 128], pt[0:nk, q0:T], first, last, [vtb, ptb], [pob])

            for blk in blk_order:
                bsegs = [s_ for s_ in segs if s_[0] // KVB == blk]
                nkeys = sum(s_[1] for s_ in bsegs)
                i = kvctr[0] % 2
                kvctr[0] += 1
                kt_, ktb = kblk[i], B("kblk%d" % i)
                vt_, vtb = vblk[i], B("vblk%d" % i)
                vfl = vflat[i]
                k0 = blk * KVB * 128
                dma("sp", kt_[:, 0:nkeys], kt_scr[h, :, k0:k0 + nkeys], [B("kt_scr")], [ktb], "ld_k%d" % i)
                nfull = sum(1 for s_ in bsegs if s_[1] == 128)
                if nfull:
                    dma("sp", vt_[:, 0:nfull, :], v_scr[:, h, blk * KVB:blk * KVB + nfull, :], [B("v_scr")], [vtb], "ld_v%d" % i)
                for s_ in bsegs:
                    if s_[1] != 128:
                        kbl = s_[0] - blk * KVB
                        dma("sp", vt_[0:s_[1], kbl:kbl + 1, :], v_scr[0:s_[1], h, s_[0]:s_[0] + 1, :], [B("v_scr")], [vtb], "ld_v%d" % i)
                for (kb, nk, q0, mk) in bsegs:
                    kbl = kb - blk * KVB
                    p, pb = psum("mm")
                    mm(p[0:nk, q0:T], kt_[:, kbl * 128:kbl * 128 + nk], qT[0:96, h, q0:T], True, True, [ktb, QT[h]], [pb])
                    ip = pctr_[0] % 4
                    pctr_[0] += 1
                    pt, ptb = pT[ip], B("pT%d" % ip)
                    act(pt[0:nk, q0:T], p[0:nk, q0:T], AF.Exp, [pb], [ptb], scale=ATT_SCALE)
                    if mk is not None:
                        mt, mtb = (mdiag, B("mdiag")) if mk == "d" else (modd, B("modd"))
                        tt("dve", pt[0:nk, q0:q0 + 128], pt[0:nk, q0:q0 + 128], mt[0:nk, :], ALU.mult, [mtb], [ptb])
                    first = (kb == ordered[0][0])
                    last = (kb == ordered[-1][0])
                    pend.append((kbl, nk, q0, vfl, vtb, pt, ptb, first, last))
                    nseg_done[0] += 1
                    if nseg_done[0] == 1 and epia_pending:
                        epia_pending.pop(0)()
                    if nseg_done[0] == min(8, len(ordered) - 1) and epi_pending:
                        epi_pending.pop(0)()
                    if nseg_done[0] == min(4, len(ordered) - 1) and h + 1 < NH:
                        qproj(h + 1)
                    if len(pend) > 2:
                        pv(pend.pop(0))
            while pend:
                pv(pend.pop(0))
            def epi_a(po=po, pob=pob, h=h):
                recip(rden[64:65, 0:T], po[64:65, 0:T], [pob], [B("rden")])

            def epi(po=po, pob=pob, h=h):
                pbc, pbcb = psum("misc")
                mm(pbc[0:64, 0:T], ones_f[64:65, 0:64], rden[64:65, 0:T], True, True, [B("rden"), B("ones_f")], [pbcb])
                cp("act", bcsb[:, 0:T], pbc[0:64, 0:T], [pbcb], [B("bcsb")])
                tt("dve", oT[:, h, 0:T], po[0:64, 0:T], bcsb[:, 0:T], ALU.mult, [pob, B("bcsb")], [B("oT")])
            epia_pending.append(epi_a)
            epi_pending.append(epi)
        while epi_pending:
            if epia_pending:
                epia_pending.pop(0)()
            epi_pending.pop(0)()
        for ob in range(4):
            w, wb = wfetch(s_o[j][ob], 64, 4096, scr[("o", j)])
            wv = w[0:64, 0:4096].rearrange("p (h n) -> p h n", h=NH)
            for oo in range(2):
                oc = 2 * ob + oo
                p, pb = psum("acc")
                for h in range(NH):
                    mm(p[:, 0:T], wv[:, h, oo * 128:(oo + 1) * 128], oT[:, h, 0:T], h == 0, h == NH - 1, [wb, B("oT")], [pb])
                tt("dve", hT[:, oc, 0:T], p[:, 0:T], hT[:, oc, 0:T], ALU.add, [pb], [HT[oc]])

    import os
    STAGE = int(os.environ.get("KSTAGE", "9"))
    for blk in range(cfg.NPC // 4 if STAGE >= 1 else 0):
        for c in range(4):
            r0 = (blk * 4 + c) * 128
            dma("sp", kvsb2[c % 2][:, :], c_kv[r0:r0 + 128, :], [], [B("kvsb%d" % (c % 2))], "ld_ckv%d" % (c % 2))
            to_kvT(kvsb2[c % 2], c * 128, 128, B("kvsb%d" % (c % 2)))
        if not os.environ.get('KNOEXP'):
            expand(512, kt_s, v_s, blk * 4)

    def phase1_tile(T, x_src, rope_src, kv_out_fn, kt_scr, v_scr, kb0, av_out, hmid_fn):
        dma("sp", hT[:, :, 0:T], x_src, [], HT, "ld_x")
        for l in range(N_A):
            ffn(T, 0, l)
            gmlp(T, l, av_out)
            ffn(T, 1, l)
        hmid_fn()
        latent(T, rope_src, kv_out_fn, kt_scr, v_scr, kb0)

    for t in range(NT1 if STAGE >= 3 else 0):
        def kvout(ci, tc, t=t):
            if ci % 2 == 0:
                my = 2 * t + ci // 2
                return o_kv_p[my * 128:(my + 1) * 128, :]
            return None

        def hm(t=t):
            t2_, half = t // 2, t % 2
            for k in range(2):
                dma("pool", hmid_p[t2_, :, :, half * 256 + k * 128:half * 256 + (k + 1) * 128],
                    hT[:, :, 2 * k * 128:(2 * k + 1) * 128], HT, [B("hmid_p")], "st_hmid")
        phase1_tile(512, x_p[:, :, t * 512:(t + 1) * 512],
                    rope_tok_p[t * 512:(t + 1) * 512, :].rearrange("(c p) k -> p c k", p=128),
                    kvout, kt_p, v_p, t * 4, None, hm)
        pump_rate[0] = 1
    pump_casts(10000)
    if STAGE >= 2:
        phase1_tile(64, x_s[:, :, :], rope_tok_s[:, :].rearrange("(c p) k -> p c k", p=64),
                    lambda ci, tc: o_kv_s[0:64, :], kt_s, v_s, cfg.NPC, o_av_s,
                    lambda: dma("pool", hmid_s[:, :, :], hT[:, :, 0:64], HT, [B("hmid_s")], "st_hmid"))

    def phase2_tile(T, h_src, hbuf, rope_src, kt_scr, v_scr, segs, y_dst):
        dma("sp", hT[:, :, 0:T], h_src, [hbuf], HT, "ld_x")
        for j in range(DEPTH - N_A):
            l = N_A + j
            ffn(T, 0, l)
            mla(T, j, rope_src, kt_scr, v_scr, segs)
            ffn(T, 1, l)
        rmsnorm_fm(T, 13 * 8, dst_is_y=True)
        dma("pool", y_dst, yfm[:, :, 0:T], [B("yfm%d" % kc) for kc in range(8)], [B("y_out")], "st_y")

    while pending_expand:
        pending_expand.pop(0)()
    allb = [B(n) for n in ['uT%d' % k for k in range(8)] + ['gated%d' % k for k in range(8)] + ['vtok0', 'vtok1', 'vnf', 'tmpg0', 'tmpg1', 'vnb0', 'vnb1', 'vnb2', 'vnb3', 'ktsb', 'vexp', 'yfm0', 'yfm1', 'yfm2', 'yfm3', 'yfm4', 'yfm5', 'yfm6', 'yfm7', 'qlat', 'ropef', 't1', 't2', 'rden', 'bcsb', 'qn0', 'qn1', 'pT0', 'pT1', 'pT2', 'pT3', 'kblk0', 'kblk1', 'vblk0', 'vblk1', 'oT']]
    sch.op("dve", lambda e: e.memset(small[:, 7:8], 0.0), writes=allb + [B("small0")])
    for i_ in range(2):
        sch.op("dve", lambda e, i_=i_: e.memset(vflat[i_][:, 0:1328], 0.0), writes=[B("vblk%d" % i_)])
    segs_s = [(kb, 128, 0, None) for kb in range(cfg.NPC)] + [(cfg.NPC, 64, 0, None)]
    if STAGE >= 4:
        phase2_tile(64, hmid_s[:, :, :], B("hmid_s"), rope_fm_s[:, :, :], kt_s, v_s, segs_s, y_s[:, :, :])
    for t in range(NT2 if STAGE >= 5 else 0):
        segs = [(kb, 128, 0, None) for kb in range(8 * t)]
        for r in range(8):
            segs.append((8 * t + r, 128, (r // 2) * 128, "d" if r % 2 == 0 else "o"))
        phase2_tile(512, hmid_p[t], B("hmid_p"), rope_fm_p[:, :, t * 512:(t + 1) * 512], kt_p, v_p, segs,
                    y_p[:, :, t * 512:(t + 1) * 512])

    finals = [(k, n) for k, n in sch.keycount.items()]
    sch.emit(es, finals)
    es.close()
    return nc, sch


def _fm(v):
    return np.ascontiguousarray(v.reshape(8, 128).T)


def _rope_tables(pos):
    half = ROPE // 2
    inv = (1.0 / (10000.0 ** (np.arange(half, dtype=np.float32) * np.float32(2.0 / ROPE)))).astype(np.float32)
    ang = pos.astype(np.float32)[:, None] * inv[None, :]
    return np.cos(ang).astype(np.float32), np.sin(ang).astype(np.float32)


_PROG_CACHE = {}


def kernel(x_prompt, x_sample, cache_ckv, cache_krope,
           ffn1_norm, ffn1_w_gu, ffn1_w_down, mix_norm, ffn2_norm, ffn2_w_gu, ffn2_w_down,
           a_w_in, a_v_norm, a_w_s, a_b_s, a_w_out,
           kv_norm, w_dkv, ckv_norm, w_uk, w_uv,
           b_w_dq, b_q_norm, b_w_uq, b_w_o, final_norm):
    f32 = np.float32
    A = lambda a: np.ascontiguousarray(np.asarray(a, dtype=f32))
    x_prompt, x_sample, cache_ckv, cache_krope = A(x_prompt), A(x_sample), A(cache_ckv), A(cache_krope)
    Bn, S, _ = x_prompt.shape
    past = cache_ckv.shape[1]
    cfg = Cfg(S, past)
    NC1 = S // 128
    NMY = NC1 // 2
    key = (S, past)
    if key not in _PROG_CACHE:
        _PROG_CACHE[key] = build_program(cfg)[0]
    nc = _PROG_CACHE[key]

    shared = {}
    gu = [A(ffn1_w_gu), A(ffn2_w_gu)]
    dn = [A(ffn1_w_down), A(ffn2_w_down)]
    for f in range(2):
        for l in range(DEPTH):
            shared["w_gu%d_%d" % (f, l)] = np.ascontiguousarray(
                gu[f][l].reshape(8, 128, 2, 11, 2, 128).transpose(3, 1, 4, 0, 2, 5)).reshape(11, 128, 4096)
            shared["w_dn%d_%d" % (f, l)] = np.ascontiguousarray(
                dn[f][l].reshape(NJ, 128, 4, 2, 128).transpose(2, 1, 3, 0, 4)).reshape(4, 128, 5632)
    a_w_in, a_w_out, a_w_s, a_b_s, a_v_norm = A(a_w_in), A(a_w_out), A(a_w_s), A(a_b_s), A(a_v_norm)
    for l in range(N_A):
        shared["w_inu%d" % l] = np.ascontiguousarray(
            a_w_in[l][:, :1024].reshape(8, 128, 2, 4, 128).transpose(2, 1, 3, 0, 4)).reshape(2, 128, 4096)
        shared["w_inv%d" % l] = np.ascontiguousarray(
            a_w_in[l][:, 1024:].reshape(8, 128, 2, 512).transpose(2, 1, 0, 3)).reshape(2, 128, 4096)
        shared["w_out%d" % l] = np.ascontiguousarray(
            a_w_out[l].reshape(8, 128, 2, 4, 128).transpose(2, 1, 3, 0, 4)).reshape(2, 128, 4096)
    shared["w_dkv"] = np.ascontiguousarray(A(w_dkv).reshape(8, 128, 160).transpose(1, 0, 2)).reshape(1, 128, 1280)
    b_w_dq, b_w_uq, b_w_o, b_q_norm = A(b_w_dq), A(b_w_uq), A(b_w_o), A(b_q_norm)
    for j in range(2):
        shared["w_dq%d" % j] = np.ascontiguousarray(
            b_w_dq[j].reshape(8, 128, 2, 128).transpose(1, 2, 0, 3)).reshape(1, 128, 2048)
        wu = b_w_uq[j].reshape(2, 128, NH, 96)
        shared["w_uq%d" % j] = np.ascontiguousarray(wu.transpose(1, 0, 2, 3)).reshape(1, 128, 3072)
        wr = np.zeros_like(wu)
        wr[..., 64:80] = wu[..., 80:96]
        wr[..., 80:96] = wu[..., 64:80]
        shared["w_uqr%d" % j] = np.ascontiguousarray(wr.transpose(1, 0, 2, 3)).reshape(1, 128, 3072)
        shared["w_o%d" % j] = np.ascontiguousarray(
            b_w_o[j].reshape(NH, 64, 4, 256).transpose(2, 1, 0, 3)).reshape(4, 64, 4096)
    shared["w_uk"] = A(w_uk)
    shared["w_uv"] = A(w_uv)
    shared["w_sT"] = np.ascontiguousarray(a_w_s.transpose(0, 3, 1, 2)).reshape(N_A, 128, 1024)
    shared["b_s"] = a_b_s.reshape(N_A, 1, 1024)
    gl = [A(ffn1_norm)[l] for l in range(DEPTH)] + [A(mix_norm)[l] for l in range(DEPTH)] + \
         [A(ffn2_norm)[l] for l in range(DEPTH)] + [A(kv_norm), A(final_norm)]
    shared["gains"] = np.ascontiguousarray(np.concatenate([_fm(g) for g in gl], axis=1))
    shared["gq"] = np.ascontiguousarray(np.concatenate([b_q_norm[j].reshape(2, 128).T for j in range(2)], axis=1))
    shared["vg"] = a_v_norm.reshape(N_A, 1, 1024)
    shared["ckvg"] = A(ckv_norm).reshape(1, 128)
    shared["ident"] = np.eye(128, dtype=f32)
    isel = np.zeros((32, 96), f32)
    isel[np.arange(32), 64 + np.arange(32)] = 1.0
    shared["isel"] = isel
    md = np.ones((128, 128), f32)
    md[64:, :64] = 0.0
    shared["mdiag"] = md
    shared["mut"] = np.triu(np.ones((128, 128), f32))
    cs, sn = _rope_tables(past + np.arange(64))
    shared["rope_tok_s"] = np.ascontiguousarray(np.concatenate([cs, cs, sn], axis=1))
    shared["rope_fm_s"] = np.ascontiguousarray(np.stack([np.concatenate([cs.T, cs.T], 0), np.concatenate([-sn.T, sn.T], 0)], axis=1))

    in_maps = []
    perms = []
    for c in range(8):
        b, par = c // 2, c % 2
        perm = np.empty(NC1, np.int64)
        perm[0::2] = np.arange(0, NC1, 2) + par
        perm[1::2] = np.arange(0, NC1, 2) + (1 - par)
        perms.append(perm)
        xl = x_prompt[b].reshape(NC1, 128, D)[perm].reshape(S, D)
        m = dict(shared)
        m["x_p"] = np.ascontiguousarray(xl.T.reshape(8, 128, S).transpose(1, 0, 2))
        m["x_s"] = np.ascontiguousarray(x_sample[c].T.reshape(8, 128, 64).transpose(1, 0, 2))
        m["c_kv"] = np.ascontiguousarray(np.concatenate([cache_ckv[c], cache_krope[c]], axis=1))
        m["modd"] = np.full((128, 128), float(par), f32)
        pos_loc = (perm[:, None] * 128 + np.arange(128)[None, :]).reshape(-1)
        cs, sn = _rope_tables(pos_loc)
        m["rope_tok_p"] = np.ascontiguousarray(np.concatenate([cs, cs, sn], axis=1))
        pos_my = ((np.arange(NMY) * 2 + par)[:, None] * 128 + np.arange(128)[None, :]).reshape(-1)
        cs, sn = _rope_tables(pos_my)
        m["rope_fm_p"] = np.ascontiguousarray(np.stack([np.concatenate([cs.T, cs.T], 0), np.concatenate([-sn.T, sn.T], 0)], axis=1))
        in_maps.append(m)

    import os as _os
    _nk = int(_os.environ.get("KCORES", "8"))
    res = run_bass_kernel_spmd(nc, in_maps[:_nk], core_ids=list(range(_nk)))
    if _nk < 8:
        res.results.extend([{k: np.zeros_like(v) for k, v in res.results[0].items()} for _ in range(8 - _nk)])
    y_prompt = np.zeros((Bn, S, D), f32)
    ckv_p = np.zeros((Bn, S, KVL), f32)
    kr_p = np.zeros((Bn, S, ROPE), f32)
    y_sample = np.zeros((8, 64, D), f32)
    ckv_s = np.zeros((8, 64, KVL), f32)
    kr_s = np.zeros((8, 64, ROPE), f32)
    av_s = np.zeros((N_A, 8, 64, D), f32)
    for c in range(8):
        b, par = c // 2, c % 2
        r = res.results[c]
        yp = np.asarray(r["y_p"]).transpose(2, 1, 0).reshape(NMY, 128, D)
        y_prompt[b].reshape(NC1, 128, D)[par::2] = yp
        kvp = np.asarray(r["o_kv_p"]).reshape(NMY, 128, 160)
        ckv_p[b].reshape(NC1, 128, KVL)[par::2] = kvp[:, :, :128]
        kr_p[b].reshape(NC1, 128, ROPE)[par::2] = kvp[:, :, 128:]
        y_sample[c] = np.asarray(r["y_s"]).transpose(2, 1, 0).reshape(64, D)
        kvs = np.asarray(r["o_kv_s"])
        ckv_s[c] = kvs[:, :128]
        kr_s[c] = kvs[:, 128:]
        av_s[:, c] = np.asarray(r["o_av_s"])
    return (y_prompt, y_sample, ckv_p, kr_p, ckv_s, kr_s, av_s)
```

```python
import math
from contextlib import ExitStack
import numpy as np
import concourse.bass as bass
import concourse.mybir as mybir
from concourse.bass_utils import run_bass_kernel_spmd

F32 = mybir.dt.float32
BF16 = mybir.dt.bfloat16
AF = mybir.ActivationFunctionType
ALU = mybir.AluOpType

D = 1024
DEPTH = 4
N_A = 2
DFF = 2816
NJ = DFF // 128
KVL = 128
ROPE = 32
NH = 16
QL = 256
EPS = 1e-6
ATT_SCALE = 96 ** -0.5
SLOT = 6144
NSLOT = 3
KVB = 16
SEM_LIMIT = 30000
SAME_ENGINE_SKIP = 1 << 30
import os as _os0
INLINE_WAIT = _os0.environ.get("KINLINE", "0") == "1"


class Buf:
    __slots__ = ("name", "w", "r", "excl")

    def __init__(self, name, excl=False):
        self.name = name
        self.w = None
        self.r = []
        self.excl = excl


class Op:
    __slots__ = ("eng", "pos", "fn", "deps", "needs_inc", "key", "kidx", "gen", "count")


class Sched:
    ENGS = ("pe", "act", "dve", "pool", "sp")

    def __init__(self, nc):
        self.nc = nc
        self.ops = {e: [] for e in self.ENGS}
        self.seen = {e: {} for e in self.ENGS}
        self.keycount = {}
        self.nops = 0

    def op(self, eng, fn, reads=(), writes=(), key=None):
        o = Op()
        o.eng = eng
        o.pos = len(self.ops[eng])
        o.fn = fn
        o.needs_inc = False
        o.key = key
        o.gen = 0
        o.count = 0
        if key is not None:
            self.keycount[key] = self.keycount.get(key, 0) + 1
            o.kidx = self.keycount[key]
        else:
            o.kidx = 0
        writes = list(writes) + [b for b in reads if b.excl]
        reads = [b for b in reads if not b.excl]
        cand = []
        wset = set(id(b) for b in writes)
        for b in reads:
            if id(b) in wset:
                continue
            if b.w is not None:
                cand.append(b.w)
        for b in writes:
            if b.w is not None:
                cand.append(b.w)
            cand.extend(b.r)
        seen = self.seen[eng]
        deps = {}
        for p in cand:
            if p.key is not None:
                stream, idx = ("k", p.key), p.kidx
            else:
                stream, idx = ("e", p.eng), p.pos
                if p.eng == eng and key is None:
                    if eng == "pe":
                        continue
                    if o.pos - p.pos >= SAME_ENGINE_SKIP:
                        continue
            if seen.get(stream, -1) >= idx:
                continue
            if stream not in deps or (deps[stream].kidx if p.key is not None else deps[stream].pos) < idx:
                deps[stream] = p
        for stream, p in deps.items():
            seen[stream] = p.kidx if p.key is not None else p.pos
            p.needs_inc = True
        o.deps = list(deps.values())
        for b in reads:
            if id(b) not in wset:
                b.r.append(o)
        for b in writes:
            b.w = o
            b.r = []
        self.ops[eng].append(o)
        self.nops += 1
        return o

    def emit(self, es, final_waits):
        nc = self.nc
        esems = {}
        for e in self.ENGS:
            gen, cnt = 0, 0
            esems[e] = [es.enter_context(nc.semaphore("s_%s_0" % e))]
            for o in self.ops[e]:
                if o.key is None and o.needs_inc:
                    if cnt >= SEM_LIMIT:
                        gen += 1
                        cnt = 0
                        esems[e].append(es.enter_context(nc.semaphore("s_%s_%d" % (e, gen))))
                    cnt += 1
                    o.gen, o.count = gen, cnt
        ksems = {k: es.enter_context(nc.semaphore("k_%s" % k)) for k in self.keycount}
        block = es.enter_context(nc.Block())
        handles = {"pe": block.tensor, "act": block.scalar, "dve": block.vector, "pool": block.gpsimd, "sp": block.sync}

        def make(e):
            def body(eh):
                for o in self.ops[e]:
                    ws = [(ksems[p.key], 16 * p.kidx) if p.key is not None else (esems[p.eng][p.gen], p.count)
                          for p in o.deps]
                    inline = ws.pop() if (ws and INLINE_WAIT) else None
                    for sem, val in ws:
                        eh.wait_ge(sem, val)
                    ins = o.fn(eh)
                    if inline is not None:
                        ins = ins._wait_ge(inline[0], inline[1])
                    if o.key is not None:
                        ins.then_inc(ksems[o.key], 16)
                    elif o.needs_inc:
                        ins.then_inc(esems[e][o.gen], 1)
                if e == "pool":
                    for k, n in final_waits:
                        eh.wait_ge(ksems[k], 16 * n)
            return body

        for e in self.ENGS:
            handles[e](make(e))


class Cfg:
    def __init__(self, S, past):
        self.S = S
        self.past = past
        self.NT1 = S // 512
        self.NT2 = S // 1024
        self.NMY = S // 256
        self.NPC = past // 128


def build_program(cfg):
    VC = 80
    nc = bass.Bass("TRN2", target_bir_lowering=False)
    S, NT1, NT2, NMY = cfg.S, cfg.NT1, cfg.NT2, cfg.NMY
    MYTOK = NMY * 128
    NKP = S // 128
    NKS = cfg.NPC + 1

    def din(name, shape):
        return nc.dram_tensor(name, list(shape), F32, kind="ExternalInput").ap()

    def dout(name, shape):
        return nc.dram_tensor(name, list(shape), F32, kind="ExternalOutput").ap()

    def dscr(name, shape, dt=BF16):
        return nc.dram_tensor(name, list(shape), dt, kind="Internal").ap()

    x_p = din("x_p", [128, 8, S])
    x_s = din("x_s", [128, 8, 64])
    c_kv = din("c_kv", [cfg.past, 160])
    w_gu = [[din("w_gu%d_%d" % (f, l), [11, 128, 4096]) for l in range(DEPTH)] for f in range(2)]
    w_dn = [[din("w_dn%d_%d" % (f, l), [4, 128, 5632]) for l in range(DEPTH)] for f in range(2)]
    w_inu = [din("w_inu%d" % l, [2, 128, 4096]) for l in range(N_A)]
    w_inv = [din("w_inv%d" % l, [2, 128, 4096]) for l in range(N_A)]
    w_out = [din("w_out%d" % l, [2, 128, 4096]) for l in range(N_A)]
    w_dkv = din("w_dkv", [1, 128, 1280])
    w_dq = [din("w_dq%d" % j, [1, 128, 2048]) for j in range(2)]
    w_uq = [din("w_uq%d" % j, [1, 128, 3072]) for j in range(2)]
    w_uqr = [din("w_uqr%d" % j, [1, 128, 3072]) for j in range(2)]
    w_o = [din("w_o%d" % j, [4, 64, 4096]) for j in range(2)]
    w_uk = din("w_uk", [128, 1024])
    w_uv = din("w_uv", [128, 1024])
    w_sT = din("w_sT", [N_A, 128, 1024])
    b_s = din("b_s", [N_A, 1, 1024])
    gains = din("gains", [128, 14 * 8])
    gq = din("gq", [128, 4])
    vg = din("vg", [N_A, 1, 1024])
    ckvg = din("ckvg", [1, 128])
    ident_d = din("ident", [128, 128])
    isel_d = din("isel", [32, 96])
    mdiag_d = din("mdiag", [128, 128])
    modd_d = din("modd", [128, 128])
    mut_d = din("mut", [128, 128])
    rope_tok_p = din("rope_tok_p", [S, 48])
    rope_tok_s = din("rope_tok_s", [64, 48])
    rope_fm_p = din("rope_fm_p", [32, 2, MYTOK])
    rope_fm_s = din("rope_fm_s", [32, 2, 64])

    y_p = dout("y_p", [128, 8, MYTOK])
    y_s = dout("y_s", [128, 8, 64])
    o_kv_p = dout("o_kv_p", [MYTOK, 160])
    o_kv_s = dout("o_kv_s", [64, 160])
    o_av_s = dout("o_av_s", [N_A, 64, D])

    s_gu = [[dscr("s_gu%d_%d" % (f, l), [11, 128, 4096]) for l in range(DEPTH)] for f in range(2)]
    s_dn = [[dscr("s_dn%d_%d" % (f, l), [4, 128, 5632]) for l in range(DEPTH)] for f in range(2)]
    s_inu = [dscr("s_inu%d" % l, [2, 128, 4096]) for l in range(N_A)]
    s_inv = [dscr("s_inv%d" % l, [2, 128, 4096]) for l in range(N_A)]
    s_out = [dscr("s_out%d" % l, [2, 128, 4096]) for l in range(N_A)]
    s_dkv = dscr("s_dkv", [1, 128, 1280])
    s_dq = [dscr("s_dq%d" % j, [1, 128, 2048]) for j in range(2)]
    s_uq = [dscr("s_uq%d" % j, [1, 128, 3072]) for j in range(2)]
    s_uqr = [dscr("s_uqr%d" % j, [1, 128, 3072]) for j in range(2)]
    s_o = [dscr("s_o%d" % j, [4, 64, 4096]) for j in range(2)]
    kt_p = dscr("kt_p", [NH, 96, NKP * 128])
    v_p = dscr("v_p", [128, NH, NKP, VC])
    kt_s = dscr("kt_s", [NH, 96, NKS * 128])
    v_s = dscr("v_s", [128, NH, NKS, VC])
    hmid_p = dscr("hmid_p", [max(NT2, 1), 128, 8, 512], F32)
    hmid_s = dscr("hmid_s", [128, 8, 64], F32)

    es = ExitStack()
    sch = Sched(nc)
    bufs = {}

    def B(name):
        if name not in bufs:
            bufs[name] = Buf(name)
        return bufs[name]

    def sb(name, shape, dt=F32):
        return es.enter_context(nc.sbuf_tensor(name, list(shape), dt))

    hT = sb("hT", [128, 8, 512])
    xn = sb("xn", [128, 8, 512], BF16)
    sq = sb("sq", [128, 8, 512], BF16)
    sd = sb("sd", [128, 512])
    rstd = sb("rstd", [128, 512])
    actb = sb("actb", [128, NJ, 512], BF16)
    sg = [sb("sg%d" % i, [128, 512]) for i in range(2)]
    bt = [sb("bt%d" % i, [128, 512]) for i in range(2)]
    U32 = sb("U32", [128, 8192])
    U16 = sb("U16", [128, 18048], BF16)

    def carve(U, off, n, pat=None, np_=128, **kw):
        v = U[0:np_, off:off + n]
        return v.rearrange(pat, **kw) if pat else v

    uT = carve(U32, 0, 4096, "p (c t) -> p c t", c=8)
    vtok = carve(U32, 4096, 1024)
    vnf = carve(U32, 5120, 1024)
    vtok2 = [vtok, carve(U32, 7168, 1024)]
    tmpg = [carve(U32, 6144 + i * 512, 512) for i in range(2)]
    gated = carve(U16, 0, 4096, "p (c t) -> p c t", c=8)
    vnb = [carve(U16, 4096 + i * 1024, 1024) for i in range(2)] + [carve(U16, 15360 + i * 1024, 1024) for i in range(2)]
    ktsb = carve(U16, 6144, 4096, "p (c t) -> p c t", np_=96, c=8)
    vexp = carve(U16, 10240, NH * 4 * VC, "p (h c d) -> p h c d", h=NH, c=4)
    yfm = carve(U32, 0, 4096, "p (c t) -> p c t", c=8)
    qlat = carve(U32, 4096, 1024, "p (c t) -> p c t", c=2)
    ropef = carve(U32, 5120, 1024, "p (c t) -> p c t", np_=96, c=2)
    t1 = carve(U32, 6144, 512, np_=96)
    t2 = carve(U32, 6656, 512, np_=96)
    rden = carve(U32, 7168, 512, np_=96)
    bcsb = carve(U32, 7680, 512, np_=64)
    qn = carve(U16, 0, 1024, "p (c t) -> p c t", c=2)
    pT = [carve(U16, 1024 + i * 512, 512) for i in range(3)] + [carve(U16, 17504, 512)]
    kblk = [carve(U16, 2560 + i * 2048, 2048, np_=96) for i in range(2)]
    vblk = [carve(U16, 6656 + i * 1328, KVB * VC, "p (k d) -> p k d", k=KVB) for i in range(2)]
    vflat = [carve(U16, 6656 + i * 1328, 1328) for i in range(2)]
    oT = carve(U16, 9312, 8192, "p (h t) -> p h t", np_=64, h=NH)
    small = sb("small", [128, 16])
    kvsb2 = [sb("kvsb%d" % i, [128, 160]) for i in range(2)]
    lat2 = [sb("lat%d" % i, [128, 160]) for i in range(2)]
    rt2 = [sb("rt%d" % i, [128, 64]) for i in range(2)]
    kvsb = kvsb2[0]
    ropet = sb("ropet", [128, 4, 48])
    ckvT = sb("ckvT", [128, 512], BF16)
    krT = sb("krT", [32, 512], BF16)
    ring = [sb("ring%d" % i, [128, SLOT], BF16) for i in range(NSLOT)]
    wsT = sb("wsT", [128, N_A, 8, 128], BF16)
    bias_bc = sb("bias_bc", [128, N_A, 1024])
    vg_bc = sb("vg_bc", [128, N_A, 1024])
    ckvg_bc = sb("ckvg_bc", [128, 128])
    wuk_ext = sb("wuk_ext", [128, NH, 96], BF16)
    wuv_b = sb("wuv_b", [128, 1024], BF16)
    ident = sb("ident_s", [128, 128])
    isel = sb("isel_s", [32, 96], BF16)
    ones_m = sb("ones_m", [128, 128], BF16)
    ones_q = sb("ones_q", [128, 128], BF16)
    ones_f = sb("ones_f", [128, 64])
    eps_t = sb("eps_t", [128, 1])
    gains_s = sb("gains_s", [128, 14 * 8])
    gq_s = sb("gq_s", [128, 4])
    mdiag = sb("mdiag_s", [128, 128], BF16)
    modd = sb("modd_s", [128, 128], BF16)
    mut = sb("mut_s", [128, 128])
    qT = actb
    stage = vtok

    ppool = {}
    for pool, n in (("mm", 4), ("acc", 2), ("misc", 2)):
        ppool[pool] = [es.enter_context(nc.psum_tensor("ps_%s%d" % (pool, i), [128, 512], F32)) for i in range(n)]
    pctr = {"mm": 0, "acc": 0, "misc": 0}

    def psum(pool):
        i = pctr[pool] % len(ppool[pool])
        pctr[pool] += 1
        nm = "ps_%s%d" % (pool, i)
        if nm not in bufs:
            bufs[nm] = Buf(nm, excl=True)
        return ppool[pool][i], bufs[nm]

    def mm(out, lhsT, rhs, start, stop, reads, writes):
        sch.op("pe", lambda e: e.matmul(out, lhsT=lhsT, rhs=rhs, start=start, stop=stop), reads=reads, writes=writes)

    def tr(out, in_, idn, reads, writes):
        sch.op("pe", lambda e: e.transpose(out, in_, idn), reads=reads, writes=writes)

    def act(out, in_, func, reads, writes, **kw):
        sch.op("act", lambda e: e.activation(out=out, in_=in_, func=func, **kw), reads=reads, writes=writes)

    def cp(eng, out, in_, reads, writes):
        if eng == "act":
            act(out, in_, AF.Copy, reads, writes)
        else:
            sch.op(eng, lambda e: e.tensor_copy(out=out, in_=in_), reads=reads, writes=writes)

    def tt(eng, out, in0, in1, op, reads, writes):
        sch.op(eng, lambda e: e.tensor_tensor(out=out, in0=in0, in1=in1, op=op), reads=reads, writes=writes)

    def stt(eng, out, in0, scalar, in1, op0, op1, reads, writes):
        sch.op(eng, lambda e: e.scalar_tensor_tensor(out=out, in0=in0, scalar=scalar, in1=in1, op0=op0, op1=op1),
               reads=reads, writes=writes)

    def recip(out, in_, reads, writes):
        sch.op("dve", lambda e: e.reciprocal(out=out, in_=in_), reads=reads, writes=writes)

    def dma(eng, out, in_, reads, writes, key):
        sch.op(eng, lambda e: e.dma_start(out=out, in_=in_), reads=reads, writes=writes, key=key)

    alt = {"n": 0}

    def evac_eng():
        alt["n"] += 1
        return "act" if alt["n"] % 2 else "dve"

    rctr = {"n": 0}

    def wfetch(src_ap, np_, n, scratch_buf):
        while cast_pending.get(id(scratch_buf), 0) > 0:
            pump_casts(1)
        i = rctr["n"] % NSLOT
        rctr["n"] += 1
        dma("sp", ring[i][0:np_, 0:n], src_ap, [scratch_buf], [B("ring%d" % i)], "ring%d" % i)
        return ring[i], B("ring%d" % i)

    sch.op("dve", lambda e: e.memset(ones_m[:], 1.0 / 1024.0), writes=[B("ones_m")])
    sch.op("dve", lambda e: e.memset(ones_q[:], 1.0 / 256.0), writes=[B("ones_q")])
    sch.op("dve", lambda e: e.memset(ones_f[:], 1.0), writes=[B("ones_f")])
    sch.op("dve", lambda e: e.memset(eps_t[:], EPS), writes=[B("eps_t")])
    sch.op("dve", lambda e: e.memset(vexp[:], 1.0), writes=[B("vexp")])
    sch.op("dve", lambda e: e.memset(wuk_ext[:], 0.0), writes=[B("wuk_ext")])

    def load_const(dst, src, name):
        dma("sp", dst, src, [], [B(name)], "c_" + name)

    load_const(gains_s[:], gains[:, :], "gains")
    load_const(gq_s[:], gq[:, :], "gq")
    load_const(ident[:], ident_d[:, :], "ident")
    load_const(mut[:], mut_d[:, :], "mut")
    for l in range(N_A):
        load_const(bias_bc[:, l, :], b_s[l, :, :].partition_broadcast(128), "bias_bc")
        load_const(vg_bc[:, l, :], vg[l, :, :].partition_broadcast(128), "vg_bc")
    load_const(ckvg_bc[:], ckvg[:, :].partition_broadcast(128), "ckvg_bc")

    def staged(src_ap, np_, n, fn):
        dma("sp", stage[0:np_, 0:n], src_ap, [], [B("vtok0")], "c_stage")
        fn()

    staged(isel_d[:, :], 32, 96, lambda: cp("dve", isel[:], stage[0:32, 0:96], [B("vtok0")], [B("isel")]))
    staged(mdiag_d[:, :], 128, 128, lambda: cp("dve", mdiag[:], stage[:, 0:128], [B("vtok0")], [B("mdiag")]))
    staged(modd_d[:, :], 128, 128, lambda: cp("dve", modd[:], stage[:, 0:128], [B("vtok0")], [B("modd")]))
    staged(w_uv[:, :], 128, 1024, lambda: cp("dve", wuv_b[:], stage[:, :], [B("vtok0")], [B("wuv_b")]))
    staged(w_uk[:, :], 128, 1024,
           lambda: cp("dve", wuk_ext[:, :, 0:64], stage[:, :].rearrange("p (h d) -> p h d", h=NH),
                      [B("vtok0")], [B("wuk_ext")]))
    for l in range(N_A):
        def f(l=l):
            for g in range(8):
                tt("dve", wsT[:, l, g, :], stage[:, g * 128:(g + 1) * 128], mut[:], ALU.mult,
                   [B("vtok0"), B("mut")], [B("wsT")])
        staged(w_sT[l], 128, 1024, f)

    cast_gate = [None]

    cast_queue = []
    cast_lazy = [False]
    cast_pending = {}

    def cast(dst, src, name, pieces):
        b = B("scr_" + name)
        for i in range(pieces):
            def f(i=i):
                dma("pool", dst[i], src[i], [B("actb")], [b], "cast_" + name)
                cast_pending[id(b)] -= 1
            if cast_lazy[0]:
                cast_pending[id(b)] = cast_pending.get(id(b), 0) + 1
                cast_queue.append(f)
            else:
                dma("pool", dst[i], src[i], [], [b], "cast_" + name)
        return b

    def pump_casts(n):
        for _ in range(n):
            if cast_queue:
                cast_queue.pop(0)()

    scr = {}

    def emit_casts(l):
        scr[("gu", 0, l)] = cast(s_gu[0][l], w_gu[0][l], "gu0_%d" % l, 11)
        cast_lazy[0] = True
        scr[("dn", 0, l)] = cast(s_dn[0][l], w_dn[0][l], "dn0_%d" % l, 4)
        if l < N_A:
            scr[("inu", l)] = cast(s_inu[l], w_inu[l], "inu%d" % l, 2)
            scr[("inv", l)] = cast(s_inv[l], w_inv[l], "inv%d" % l, 2)
            scr[("out", l)] = cast(s_out[l], w_out[l], "out%d" % l, 2)
        else:
            j = l - N_A
            scr[("dq", j)] = cast(s_dq[j], w_dq[j], "dq%d" % j, 1)
            scr[("uq", j)] = cast(s_uq[j], w_uq[j], "uq%d" % j, 1)
            scr[("uqr", j)] = cast(s_uqr[j], w_uqr[j], "uqr%d" % j, 1)
            scr[("o", j)] = cast(s_o[j], w_o[j], "o%d" % j, 4)
        scr[("gu", 1, l)] = cast(s_gu[1][l], w_gu[1][l], "gu1_%d" % l, 11)
        scr[("dn", 1, l)] = cast(s_dn[1][l], w_dn[1][l], "dn1_%d" % l, 4)
        if l == N_A - 1:
            scr[("dkv",)] = cast(s_dkv, w_dkv, "dkv", 1)

    emit_casts(0)
    cast_lazy[0] = True
    for l_ in range(1, DEPTH):
        emit_casts(l_)
    pump_rate = [8]

    HT = [B("hT%d" % kc) for kc in range(8)]

    def rmsnorm_fm(T, gcol, nchunks=8, src=None, dst=None, ones=None, gtile=None, dst_is_y=False):
        src = hT if src is None else src
        dst = xn if dst is None else dst
        if dst_is_y:
            dst = yfm
        ones = ones_m if ones is None else ones
        gtile = gains_s if gtile is None else gtile
        sbufs = HT if src is hT else [B("qlat")] * 8
        dname = "xn" if dst is xn else ("qn" if dst is qn else "yfm")
        p, pb = psum("misc")
        for kc in range(nchunks):
            if kc % 2 == 0:
                act(sq[:, kc, 0:T], src[:, kc, 0:T], AF.Square, [sbufs[kc]], [B("sq%d" % kc)])
            else:
                tt("pool", sq[:, kc, 0:T], src[:, kc, 0:T], src[:, kc, 0:T], ALU.mult, [sbufs[kc]], [B("sq%d" % kc)])
            mm(p[:, 0:T], ones[:, :], sq[:, kc, 0:T], kc == 0, kc == nchunks - 1, [B("sq%d" % kc), B("ones_m"), B("ones_q")], [pb])
        act(sd[:, 0:T], p[:, 0:T], AF.Sqrt, [pb, B("eps_t")], [B("sd")], bias=eps_t[:, 0:1], scale=1.0)
        recip(rstd[:, 0:T], sd[:, 0:T], [B("sd")], [B("rstd")])
        for kc in range(nchunks):
            stt("dve", dst[:, kc, 0:T], src[:, kc, 0:T], gtile[:, gcol + kc:gcol + kc + 1], rstd[:, 0:T], ALU.mult, ALU.mult,
                [sbufs[kc], B("rstd"), B("gains"), B("gq")], [B("%s%d" % (dname, kc))])

    XN = [B("xn%d" % kc) for kc in range(8)]

    pending_expand = []

    def norm_deferred(T, gcol):
        p, pb = psum("misc")
        for kc in range(8):
            if kc % 2 == 0:
                act(sq[:, kc, 0:T], hT[:, kc, 0:T], AF.Square, [HT[kc]], [B("sq%d" % kc)])
            else:
                tt("pool", sq[:, kc, 0:T], hT[:, kc, 0:T], hT[:, kc, 0:T], ALU.mult, [HT[kc]], [B("sq%d" % kc)])
            g_ = gains_s[:, gcol + kc:gcol + kc + 1]
            if kc % 2 == 1:
                act(xn[:, kc, 0:T], hT[:, kc, 0:T], AF.Copy, [HT[kc], B("gains")], [XN[kc]], scale=g_)
            else:
                sch.op("dve", lambda e, kc=kc, g_=g_: e.tensor_scalar_mul(xn[:, kc, 0:T], hT[:, kc, 0:T], g_),
                       reads=[HT[kc], B("gains")], writes=[XN[kc]])
            mm(p[:, 0:T], ones_m[:, :], sq[:, kc, 0:T], kc == 0, kc == 7, [B("sq%d" % kc), B("ones_m")], [pb])
        act(sd[:, 0:T], p[:, 0:T], AF.Sqrt, [pb, B("eps_t")], [B("sd")], bias=eps_t[:, 0:1], scale=1.0)
        recip(rstd[:, 0:T], sd[:, 0:T], [B("sd")], [B("rstd")])

    def ffn(T, f, l):
        norm_deferred(T, ((0 if f == 0 else 8) + l) * 8)
        sb_gu = scr[("gu", f, l)]
        pump_casts(pump_rate[0])
        for jb in range(11):
            w, wb = wfetch(s_gu[f][l][jb], 128, 4096, sb_gu)
            wv = w[:, 0:4096].rearrange("p (jj kc gu m) -> p jj kc gu m", jj=2, kc=8, gu=2)
            banks = [(psum("mm"), psum("mm")) for jj in range(2)]
            if jb == 0:
                for kc in range(8):
                    for jj in range(2):
                        (pg, pgb), (pu, pub) = banks[jj]
                        mm(pg[:, 0:T], wv[:, jj, kc, 0, :], xn[:, kc, 0:T], kc == 0, kc == 7, [wb, XN[kc]], [pgb])
                        mm(pu[:, 0:T], wv[:, jj, kc, 1, :], xn[:, kc, 0:T], kc == 0, kc == 7, [wb, XN[kc]], [pub])
            for jj in range(2):
                j = 2 * jb + jj
                (pg, pgb), (pu, pub) = banks[jj]
                if jb != 0:
                    for kc in range(8):
                        mm(pg[:, 0:T], wv[:, jj, kc, 0, :], xn[:, kc, 0:T], kc == 0, kc == 7, [wb, XN[kc]], [pgb])
                    for kc in range(8):
                        mm(pu[:, 0:T], wv[:, jj, kc, 1, :], xn[:, kc, 0:T], kc == 0, kc == 7, [wb, XN[kc]], [pub])
                s_, sgb = sg[j % 2], B("sg%d" % (j % 2))
                b_, btb = bt[j % 2], B("bt%d" % (j % 2))
                tt("dve", s_[:, 0:T], pg[:, 0:T], rstd[:, 0:T], ALU.mult, [pgb, B("rstd")], [sgb])
                act(s_[:, 0:T], s_[:, 0:T], AF.Silu, [], [sgb])
                tt("dve", b_[:, 0:T], pu[:, 0:T], rstd[:, 0:T], ALU.mult, [pub, B("rstd")], [btb])
                tt("pool", actb[:, j, 0:T], b_[:, 0:T], s_[:, 0:T], ALU.mult, [btb, sgb], [B("actb")])
        while pending_expand:
            pending_expand.pop(0)()
        sb_dn = scr[("dn", f, l)]
        pump_casts(pump_rate[0])
        for ob in range(4):
            w, wb = wfetch(s_dn[f][l][ob], 128, 5632, sb_dn)
            wv = w[:, 0:5632].rearrange("p (oo j m) -> p oo j m", oo=2, j=NJ)
            for oo in range(2):
                oc = 2 * ob + oo
                py, pyb = psum("acc")
                for j in range(NJ):
                    mm(py[:, 0:T], wv[:, oo, j, :], actb[:, j, 0:T], j == 0, j == NJ - 1, [wb, B("actb")], [pyb])
                stt("dve", hT[:, oc, 0:T], py[:, 0:T], 0.5, hT[:, oc, 0:T], ALU.mult, ALU.add, [pyb], [HT[oc]])

    def chunks_of(T):
        return [(c * 128, min(128, T - c * 128)) for c in range((T + 127) // 128)]

    def gmlp(T, l, av_out):
        rmsnorm_fm(T, (4 + l) * 8)
        chs = chunks_of(T)
        wvs = []
        for nb in range(2):
            w, wb = wfetch(s_inv[l][nb], 128, 4096, scr[("inv", l)])
            wvs.append((w[:, 0:4096].rearrange("p (kc n) -> p kc n", kc=8), wb))
        for ci, (c0, tc) in enumerate(chs):
            vt_, vtb_ = vtok2[ci % 2], B("vtok%d" % (ci % 2))
            pbs = [psum("mm") for nb in range(2)]
            if ci == 0:
                for kc in range(8):
                    for nb in range(2):
                        mm(pbs[nb][0][0:tc, :], xn[:, kc, c0:c0 + tc], wvs[nb][0][:, kc, :], kc == 0, kc == 7, [wvs[nb][1], XN[kc]], [pbs[nb][1]])
            for nb in range(2):
                p, pb = pbs[nb]
                wv, wb = wvs[nb]
                if ci != 0:
                    for kc in range(8):
                        mm(p[0:tc, :], xn[:, kc, c0:c0 + tc], wv[:, kc, :], kc == 0, kc == 7, [wb, XN[kc]], [pb])
                act(vt_[0:tc, nb * 512:(nb + 1) * 512], p[0:tc, :], AF.Gelu, [pb], [vtb_])
            sm = 8 * (ci % 2)
            smb = B("small%d" % (ci % 2))
            act(vnf[0:tc, :], vt_[0:tc, :], AF.Square, [vtb_], [B("vnf"), smb], accum_out=small[0:tc, sm:sm + 1])
            act(small[0:tc, sm + 1:sm + 2], small[0:tc, sm:sm + 1], AF.Sqrt, [smb, B("eps_t")], [smb], bias=eps_t[0:tc, 0:1], scale=1.0 / 1024.0)
            recip(small[0:tc, sm + 2:sm + 3], small[0:tc, sm + 1:sm + 2], [smb], [smb])
            vb = vnb[ci % 4]
            vbb = B("vnb%d" % (ci % 4))
            if av_out is not None:
                stt("dve", vnf[0:tc, :], vt_[0:tc, :], small[0:tc, sm + 2:sm + 3], vg_bc[0:tc, l, :], ALU.mult, ALU.mult,
                    [vtb_, smb, B("vg_bc")], [B("vnf")])
                dma("pool", av_out[l, c0:c0 + tc, :], vnf[0:tc, :], [B("vnf")], [B("o_av")], "st_av")
                cp("dve", vb[0:tc, :], vnf[0:tc, :], [B("vnf")], [vbb])
            else:
                stt("dve", vb[0:tc, :], vt_[0:tc, :], small[0:tc, sm + 2:sm + 3], vg_bc[0:tc, l, :], ALU.mult, ALU.mult,
                    [vtb_, smb, B("vg_bc")], [vbb])
        for ob in range(2):
            w, wb = wfetch(s_inu[l][ob], 128, 4096, scr[("inu", l)])
            wv = w[:, 0:4096].rearrange("p (oo kc m) -> p oo kc m", oo=4, kc=8)
            for oo in range(4):
                oc = 4 * ob + oo
                p, pb = psum("mm")
                for kc in range(8):
                    mm(p[:, 0:T], wv[:, oo, kc, :], xn[:, kc, 0:T], kc == 0, kc == 7, [wb, XN[kc]], [pb])
                act(uT[:, oc, 0:T], p[:, 0:T], AF.Gelu, [pb], [B("uT%d" % oc)])
        for ci, (c0, tc) in enumerate(chs):
            vb = vnb[ci % 4]
            vbb = B("vnb%d" % (ci % 4))
            for half in range(2):
                p, pb = psum("mm")
                for gi in range(4):
                    g = half * 4 + gi
                    mm(p[:, gi * 128:gi * 128 + tc], vb[0:tc, g * 128:(g + 1) * 128], wsT[0:tc, l, g, 0:tc], True, True,
                       [vbb, B("wsT")], [pb])
                tg = tmpg[half]
                tgb = B("tmpg%d" % half)
                pv = p[:, :].rearrange("p (g t) -> p g t", g=4)[:, :, 0:tc]
                tv = tg[:, :].rearrange("p (g t) -> p g t", g=4)[:, :, 0:tc]
                bv = bias_bc[:, l, half * 512:(half + 1) * 512].rearrange("p (g t) -> p g t", g=4)[:, :, 0:tc]
                tt("dve", tv, pv, bv, ALU.add, [pb, B("bias_bc")], [tgb])
                tt("dve" if half == 0 else "pool", gated[:, half * 4:half * 4 + 4, c0:c0 + tc], tv, uT[:, half * 4:half * 4 + 4, c0:c0 + tc], ALU.mult,
                   [tgb] + [B("uT%d" % (half * 4 + k)) for k in range(4)], [B("gated%d" % (half * 4 + k)) for k in range(4)])
        for ob in range(2):
            w, wb = wfetch(s_out[l][ob], 128, 4096, scr[("out", l)])
            wv = w[:, 0:4096].rearrange("p (oo kc m) -> p oo kc m", oo=4, kc=8)
            for oo in range(4):
                oc = 4 * ob + oo
                p, pb = psum("acc")
                for g in range(8):
                    mm(p[:, 0:T], wv[:, oo, g, :], gated[:, g, 0:T], g == 0, g == 7, [wb, B("gated%d" % g)], [pb])
                tt("dve", hT[:, oc, 0:T], p[:, 0:T], hT[:, oc, 0:T], ALU.add, [pb], [HT[oc]])

    def to_kvT(src, c0, tc, srcbuf):
        import os
        dbg = int(os.environ.get("KDBG", "15"))
        p, pb = psum("misc")
        if dbg & 1:
            tr(p[:, 0:tc], src[0:tc, 0:128], ident[0:tc, 0:tc], [srcbuf, B("ident")], [pb])
        if dbg & 2:
            tr(p[0:32, 128:128 + tc], src[0:tc, 128:160], ident[0:tc, 0:tc], [srcbuf, B("ident")], [pb])
        if dbg & 4:
            cp("act", ckvT[:, c0:c0 + tc], p[:, 0:tc], [pb], [B("ckvT")])
        if dbg & 8:
            cp("dve", krT[:, c0:c0 + tc], p[0:32, 128:128 + tc], [pb], [B("krT")])

    def expand(T, kt_scr, v_scr, kb0):
        for hg in range(2):
            for hh in range(8):
                h = hg * 8 + hh
                p, pb = psum("mm")
                mm(p[0:96, 0:T], wuk_ext[:, h, :], ckvT[:, 0:T], True, False, [B("wuk_ext"), B("ckvT")], [pb])
                mm(p[0:96, 0:T], isel[:, :], krT[:, 0:T], False, True, [B("isel"), B("krT")], [pb])
                cp(evac_eng(), ktsb[:, hh, 0:T], p[0:96, 0:T], [pb], [B("ktsb")])
            dst = kt_scr[hg * 8:hg * 8 + 8, :, kb0 * 128:kb0 * 128 + T].rearrange("h p t -> p h t")
            dma("pool", dst, ktsb[:, :, 0:T], [B("ktsb")], [B("kt_scr")], "st_kt")
        chs = chunks_of(T)
        for ci, (c0, tc) in enumerate(chs):
            for nb in range(2):
                p, pb = psum("mm")
                mm(p[0:tc, :], ckvT[:, c0:c0 + tc], wuv_b[:, nb * 512:(nb + 1) * 512], True, True, [B("ckvT"), B("wuv_b")], [pb])
                cp(evac_eng(), vexp[0:tc, nb * 8:(nb + 1) * 8, ci, 0:64], p[0:tc, :].rearrange("p (h d) -> p h d", h=8), [pb], [B("vexp")])
        tcl = chs[-1][1]
        if tcl == 128:
            dma("pool", v_scr[:, :, kb0:kb0 + len(chs), :], vexp[:, :, 0:len(chs), :], [B("vexp")], [B("v_scr")], "st_v")
        else:
            dma("pool", v_scr[0:tcl, :, kb0:kb0 + 1, :], vexp[0:tcl, :, 0:1, :], [B("vexp")], [B("v_scr")], "st_v")

    def latent(T, rope_src, kv_out_fn, kt_scr, v_scr, kb0):
        rmsnorm_fm(T, 12 * 8)
        w, wb = wfetch(s_dkv[0], 128, 1280, scr[("dkv",)])
        wv = w[:, 0:1280].rearrange("p (kc n) -> p kc n", kc=8)
        chs = chunks_of(T)
        dma("sp", ropet[0:min(T, 128), 0:len(chs), :], rope_src, [], [B("ropet")], "ld_ropet")
        for ci, (c0, tc) in enumerate(chs):
            kv_, kvb = kvsb2[ci % 2], B("kvsb%d" % (ci % 2))
            la_, lab = lat2[ci % 2], B("lat%d" % (ci % 2))
            r_, rb = rt2[ci % 2], B("rt%d" % (ci % 2))
            sm = 8 * (ci % 2)
            smb = B("small%d" % (ci % 2))
            p, pb = psum("misc")
            for kc in range(8):
                mm(p[0:tc, 0:160], xn[:, kc, c0:c0 + tc], wv[:, kc, :], kc == 0, kc == 7, [wb, XN[kc]], [pb])
            cp("act", kv_[0:tc, :], p[0:tc, 0:160], [pb], [kvb])
            act(la_[0:tc, 0:128], kv_[0:tc, 0:128], AF.Square, [kvb], [lab, smb], accum_out=small[0:tc, sm + 4:sm + 5])
            act(small[0:tc, sm + 5:sm + 6], small[0:tc, sm + 4:sm + 5], AF.Sqrt, [smb, B("eps_t")], [smb], bias=eps_t[0:tc, 0:1], scale=1.0 / 128.0)
            recip(small[0:tc, sm + 6:sm + 7], small[0:tc, sm + 5:sm + 6], [smb], [smb])
            stt("dve", la_[0:tc, 0:128], kv_[0:tc, 0:128], small[0:tc, sm + 6:sm + 7], ckvg_bc[0:tc, :], ALU.mult, ALU.mult,
                [kvb, smb, B("ckvg_bc")], [lab])
            tt("dve", r_[0:tc, 0:32], kv_[0:tc, 128:160], ropet[0:tc, ci, 0:32], ALU.mult, [kvb, B("ropet")], [rb])
            tt("pool", r_[0:tc, 32:48], kv_[0:tc, 144:160], ropet[0:tc, ci, 32:48], ALU.mult, [kvb, B("ropet")], [rb])
            tt("pool", r_[0:tc, 48:64], kv_[0:tc, 128:144], ropet[0:tc, ci, 32:48], ALU.mult, [kvb, B("ropet")], [rb])
            tt("dve", la_[0:tc, 128:144], r_[0:tc, 0:16], r_[0:tc, 32:48], ALU.subtract, [rb], [lab])
            tt("dve", la_[0:tc, 144:160], r_[0:tc, 16:32], r_[0:tc, 48:64], ALU.add, [rb], [lab])
            dst = kv_out_fn(ci, tc)
            if dst is not None:
                dma("pool", dst, la_[0:tc, :], [lab], [B("o_kv")], "st_kv%d" % (ci % 2))
            to_kvT(la_, c0, tc, lab)
        pending_expand.append(lambda: expand(T, kt_scr, v_scr, kb0))

    def mla(T, j, rope_src, kt_scr, v_scr, segs):
        l = N_A + j
        norm_deferred(T, (4 + l) * 8)
        dma("sp", ropef[64:96, :, 0:T], rope_src, [], [B("ropef")], "ld_ropef")
        w, wb = wfetch(s_dq[j][0], 128, 2048, scr[("dq", j)])
        wv = w[:, 0:2048].rearrange("p (oc kc m) -> p oc kc m", oc=2, kc=8)
        for oc in range(2):
            p, pb = psum("mm")
            for kc in range(8):
                mm(p[:, 0:T], wv[:, oc, kc, :], xn[:, kc, 0:T], kc == 0, kc == 7, [wb, XN[kc]], [pb])
            tt("dve", qlat[:, oc, 0:T], p[:, 0:T], rstd[:, 0:T], ALU.mult, [pb, B("rstd")], [B("qlat")])
        rmsnorm_fm(T, j * 2, nchunks=2, src=qlat, dst=qn, ones=ones_q, gtile=gq_s)
        QN = [B("qn0"), B("qn1")]
        w1, wb1 = wfetch(s_uq[j][0], 128, 3072, scr[("uq", j)])
        w2, wb2 = wfetch(s_uqr[j][0], 128, 3072, scr[("uqr", j)])
        wv1 = w1[:, 0:3072].rearrange("p (kc h m) -> p kc h m", kc=2, h=NH)
        wv2 = w2[:, 0:3072].rearrange("p (kc h m) -> p kc h m", kc=2, h=NH)
        QT = [B("qT%d" % h) for h in range(NH)]

        def qproj(h):
            pq, pqb = psum("misc")
            pr, prb = psum("misc")
            for kc in range(2):
                mm(pq[0:128, 0:T], w1[:, kc * 1536 + h * 96:kc * 1536 + h * 96 + 128], qn[:, kc, 0:T], kc == 0, kc == 1, [wb1, QN[kc]], [pqb])
            for kc in range(2):
                mm(pr[0:128, 0:T], w2[:, kc * 1536 + h * 96:kc * 1536 + h * 96 + 128], qn[:, kc, 0:T], kc == 0, kc == 1, [wb2, QN[kc]], [prb])
            cp("act", qT[0:64, h, 0:T], pq[0:64, 0:T], [pqb], [QT[h]] + ([B("actb")] if h == 0 else []))
            tt("dve", t1[64:96, 0:T], pq[64:96, 0:T], ropef[64:96, 0, 0:T], ALU.mult, [pqb, B("ropef")], [B("t1")])
            tt("dve", t2[64:96, 0:T], pr[64:96, 0:T], ropef[64:96, 1, 0:T], ALU.mult, [prb, B("ropef")], [B("t2")])
            tt("pool", qT[64:96, h, 0:T], t1[64:96, 0:T], t2[64:96, 0:T], ALU.add, [B("t1"), B("t2")], [QT[h]])

        qproj(0)
        nblk = (segs[-1][0] // KVB) + 1
        blk_order = [nblk - 1] + list(range(nblk - 1))
        ordered = [s_ for blk in blk_order for s_ in segs if s_[0] // KVB == blk]
        assert ordered[0][2] == 0
        kvctr = [0]
        pctr_ = [0]
        epi_pending = []
        epia_pending = []
        for h in range(NH):
            po, pob = psum("acc")
            pend = []
            nseg_done = [0]

            def pv(item):
                (kbl, nk, q0, vt, vtb, pt, ptb, first, last) = item
                mm(po[0:128, q0:T], vt[0:nk, kbl * VC:kbl * VC + 128], pt[0:nk, q0:T], first, last, [vtb, ptb], [pob])

            for blk in blk_order:
                bsegs = [s_ for s_ in segs if s_[0] // KVB == blk]
                nkeys = sum(s_[1] for s_ in bsegs)
                i = kvctr[0] % 2
                kvctr[0] += 1
                kt_, ktb = kblk[i], B("kblk%d" % i)
                vt_, vtb = vblk[i], B("vblk%d" % i)
                vfl = vflat[i]
                k0 = blk * KVB * 128
                dma("sp", kt_[:, 0:nkeys], kt_scr[h, :, k0:k0 + nkeys], [B("kt_scr")], [ktb], "ld_k%d" % i)
                nfull = sum(1 for s_ in bsegs if s_[1] == 128)
                if nfull:
                    dma("sp", vt_[:, 0:nfull, :], v_scr[:, h, blk * KVB:blk * KVB + nfull, :], [B("v_scr")], [vtb], "ld_v%d" % i)
                for s_ in bsegs:
                    if s_[1] != 128:
                        kbl = s_[0] - blk * KVB
                        dma("sp", vt_[0:s_[1], kbl:kbl + 1, :], v_scr[0:s_[1], h, s_[0]:s_[0] + 1, :], [B("v_scr")], [vtb], "ld_v%d" % i)
                for (kb, nk, q0, mk) in bsegs:
                    kbl = kb - blk * KVB
                    p, pb = psum("mm")
                    mm(p[0:nk, q0:T], kt_[:, kbl * 128:kbl * 128 + nk], qT[0:96, h, q0:T], True, True, [ktb, QT[h]], [pb])
                    ip = pctr_[0] % 4
                    pctr_[0] += 1
                    pt, ptb = pT[ip], B("pT%d" % ip)
                    act(pt[0:nk, q0:T], p[0:nk, q0:T], AF.Exp, [pb], [ptb], scale=ATT_SCALE)
                    if mk is not None:
                        mt, mtb = (mdiag, B("mdiag")) if mk == "d" else (modd, B("modd"))
                        tt("dve", pt[0:nk, q0:q0 + 128], pt[0:nk, q0:q0 + 128], mt[0:nk, :], ALU.mult, [mtb], [ptb])
                    first = (kb == ordered[0][0])
                    last = (kb == ordered[-1][0])
                    pend.append((kbl, nk, q0, vfl, vtb, pt, ptb, first, last))
                    nseg_done[0] += 1
                    if nseg_done[0] == 1 and epia_pending:
                        epia_pending.pop(0)()
                    if nseg_done[0] == min(8, len(ordered) - 1) and epi_pending:
                        epi_pending.pop(0)()
                    if nseg_done[0] == min(4, len(ordered) - 1) and h + 1 < NH:
                        qproj(h + 1)
                    if len(pend) > 3:
                        pv(pend.pop(0))
            while pend:
                pv(pend.pop(0))
            def epi_a(po=po, pob=pob, h=h):
                recip(rden[64:65, 0:T], po[64:65, 0:T], [pob], [B("rden")])

            def epi(po=po, pob=pob, h=h):
                pbc, pbcb = psum("misc")
                mm(pbc[0:64, 0:T], ones_f[64:65, 0:64], rden[64:65, 0:T], True, True, [B("rden"), B("ones_f")], [pbcb])
                cp("act", bcsb[:, 0:T], pbc[0:64, 0:T], [pbcb], [B("bcsb")])
                tt("dve", oT[:, h, 0:T], po[0:64, 0:T], bcsb[:, 0:T], ALU.mult, [pob, B("bcsb")], [B("oT")])
            epia_pending.append(epi_a)
            epi_pending.append(epi)
        while epi_pending:
            if epia_pending:
                epia_pending.pop(0)()
            epi_pending.pop(0)()
        for ob in range(4):
            w, wb = wfetch(s_o[j][ob], 64, 4096, scr[("o", j)])
            wv = w[0:64, 0:4096].rearrange("p (h n) -> p h n", h=NH)
            for oo in range(2):
                oc = 2 * ob + oo
                p, pb = psum("acc")
                for h in range(NH):
                    mm(p[:, 0:T], wv[:, h, oo * 128:(oo + 1) * 128], oT[:, h, 0:T], h == 0, h == NH - 1, [wb, B("oT")], [pb])
                tt("dve", hT[:, oc, 0:T], p[:, 0:T], hT[:, oc, 0:T], ALU.add, [pb], [HT[oc]])

    import os
    STAGE = int(os.environ.get("KSTAGE", "9"))
    for blk in range(cfg.NPC // 4 if STAGE >= 1 else 0):
        for c in range(4):
            r0 = (blk * 4 + c) * 128
            dma("sp", kvsb2[c % 2][:, :], c_kv[r0:r0 + 128, :], [], [B("kvsb%d" % (c % 2))], "ld_ckv%d" % (c % 2))
            to_kvT(kvsb2[c % 2], c * 128, 128, B("kvsb%d" % (c % 2)))
        if not os.environ.get('KNOEXP'):
            expand(512, kt_s, v_s, blk * 4)

    def phase1_tile(T, x_src, rope_src, kv_out_fn, kt_scr, v_scr, kb0, av_out, hmid_fn):
        dma("sp", hT[:, :, 0:T], x_src, [], HT, "ld_x")
        for l in range(N_A):
            ffn(T, 0, l)
            gmlp(T, l, av_out)
            ffn(T, 1, l)
        hmid_fn()
        latent(T, rope_src, kv_out_fn, kt_scr, v_scr, kb0)

    for t in range(NT1 if STAGE >= 3 else 0):
        def kvout(ci, tc, t=t):
            if ci % 2 == 0:
                my = 2 * t + ci // 2
                return o_kv_p[my * 128:(my + 1) * 128, :]
            return None

        def hm(t=t):
            t2_, half = t // 2, t % 2
            for k in range(2):
                dma("pool", hmid_p[t2_, :, :, half * 256 + k * 128:half * 256 + (k + 1) * 128],
                    hT[:, :, 2 * k * 128:(2 * k + 1) * 128], HT, [B("hmid_p")], "st_hmid")
        phase1_tile(512, x_p[:, :, t * 512:(t + 1) * 512],
                    rope_tok_p[t * 512:(t + 1) * 512, :].rearrange("(c p) k -> p c k", p=128),
                    kvout, kt_p, v_p, t * 4, None, hm)
        pump_rate[0] = 1
    pump_casts(10000)
    if STAGE >= 2:
        phase1_tile(64, x_s[:, :, :], rope_tok_s[:, :].rearrange("(c p) k -> p c k", p=64),
                    lambda ci, tc: o_kv_s[0:64, :], kt_s, v_s, cfg.NPC, o_av_s,
                    lambda: dma("pool", hmid_s[:, :, :], hT[:, :, 0:64], HT, [B("hmid_s")], "st_hmid"))

    def phase2_tile(T, h_src, hbuf, rope_src, kt_scr, v_scr, segs, y_dst):
        dma("sp", hT[:, :, 0:T], h_src, [hbuf], HT, "ld_x")
        for j in range(DEPTH - N_A):
            l = N_A + j
            ffn(T, 0, l)
            mla(T, j, rope_src, kt_scr, v_scr, segs)
            ffn(T, 1, l)
        rmsnorm_fm(T, 13 * 8, dst_is_y=True)
        dma("pool", y_dst, yfm[:, :, 0:T], [B("yfm%d" % kc) for kc in range(8)], [B("y_out")], "st_y")

    while pending_expand:
        pending_expand.pop(0)()
    allb = [B(n) for n in ['uT%d' % k for k in range(8)] + ['gated%d' % k for k in range(8)] + ['vtok0', 'vtok1', 'vnf', 'tmpg0', 'tmpg1', 'vnb0', 'vnb1', 'vnb2', 'vnb3', 'ktsb', 'vexp', 'yfm0', 'yfm1', 'yfm2', 'yfm3', 'yfm4', 'yfm5', 'yfm6', 'yfm7', 'qlat', 'ropef', 't1', 't2', 'rden', 'bcsb', 'qn0', 'qn1', 'pT0', 'pT1', 'pT2', 'pT3', 'kblk0', 'kblk1', 'vblk0', 'vblk1', 'oT']]
    sch.op("dve", lambda e: e.memset(small[:, 7:8], 0.0), writes=allb + [B("small0")])
    for i_ in range(2):
        sch.op("dve", lambda e, i_=i_: e.memset(vflat[i_][:, 0:1328], 0.0), writes=[B("vblk%d" % i_)])
    segs_s = [(kb, 128, 0, None) for kb in range(cfg.NPC)] + [(cfg.NPC, 64, 0, None)]
    if STAGE >= 4:
        phase2_tile(64, hmid_s[:, :, :], B("hmid_s"), rope_fm_s[:, :, :], kt_s, v_s, segs_s, y_s[:, :, :])
    for t in range(NT2 if STAGE >= 5 else 0):
        segs = [(kb, 128, 0, None) for kb in range(8 * t)]
        for r in range(8):
            segs.append((8 * t + r, 128, (r // 2) * 128, "d" if r % 2 == 0 else "o"))
        phase2_tile(512, hmid_p[t], B("hmid_p"), rope_fm_p[:, :, t * 512:(t + 1) * 512], kt_p, v_p, segs,
                    y_p[:, :, t * 512:(t + 1) * 512])

    finals = [(k, n) for k, n in sch.keycount.items()]
    sch.emit(es, finals)
    es.close()
    return nc, sch


def _fm(v):
    return np.ascontiguousarray(v.reshape(8, 128).T)


def _rope_tables(pos):
    half = ROPE // 2
    inv = (1.0 / (10000.0 ** (np.arange(half, dtype=np.float32) * np.float32(2.0 / ROPE)))).astype(np.float32)
    ang = pos.astype(np.float32)[:, None] * inv[None, :]
    return np.cos(ang).astype(np.float32), np.sin(ang).astype(np.float32)


_PROG_CACHE = {}


def kernel(x_prompt, x_sample, cache_ckv, cache_krope,
           ffn1_norm, ffn1_w_gu, ffn1_w_down, mix_norm, ffn2_norm, ffn2_w_gu, ffn2_w_down,
           a_w_in, a_v_norm, a_w_s, a_b_s, a_w_out,
           kv_norm, w_dkv, ckv_norm, w_uk, w_uv,
           b_w_dq, b_q_norm, b_w_uq, b_w_o, final_norm):
    f32 = np.float32
    A = lambda a: np.ascontiguousarray(np.asarray(a, dtype=f32))
    x_prompt, x_sample, cache_ckv, cache_krope = A(x_prompt), A(x_sample), A(cache_ckv), A(cache_krope)
    Bn, S, _ = x_prompt.shape
    past = cache_ckv.shape[1]
    cfg = Cfg(S, past)
    NC1 = S // 128
    NMY = NC1 // 2
    key = (S, past)
    if key not in _PROG_CACHE:
        _PROG_CACHE[key] = build_program(cfg)[0]
    nc = _PROG_CACHE[key]

    shared = {}
    gu = [A(ffn1_w_gu), A(ffn2_w_gu)]
    dn = [A(ffn1_w_down), A(ffn2_w_down)]
    for f in range(2):
        for l in range(DEPTH):
            shared["w_gu%d_%d" % (f, l)] = np.ascontiguousarray(
                gu[f][l].reshape(8, 128, 2, 11, 2, 128).transpose(3, 1, 4, 0, 2, 5)).reshape(11, 128, 4096)
            shared["w_dn%d_%d" % (f, l)] = np.ascontiguousarray(
                dn[f][l].reshape(NJ, 128, 4, 2, 128).transpose(2, 1, 3, 0, 4)).reshape(4, 128, 5632)
    a_w_in, a_w_out, a_w_s, a_b_s, a_v_norm = A(a_w_in), A(a_w_out), A(a_w_s), A(a_b_s), A(a_v_norm)
    for l in range(N_A):
        shared["w_inu%d" % l] = np.ascontiguousarray(
            a_w_in[l][:, :1024].reshape(8, 128, 2, 4, 128).transpose(2, 1, 3, 0, 4)).reshape(2, 128, 4096)
        shared["w_inv%d" % l] = np.ascontiguousarray(
            a_w_in[l][:, 1024:].reshape(8, 128, 2, 512).transpose(2, 1, 0, 3)).reshape(2, 128, 4096)
        shared["w_out%d" % l] = np.ascontiguousarray(
            a_w_out[l].reshape(8, 128, 2, 4, 128).transpose(2, 1, 3, 0, 4)).reshape(2, 128, 4096)
    shared["w_dkv"] = np.ascontiguousarray(A(w_dkv).reshape(8, 128, 160).transpose(1, 0, 2)).reshape(1, 128, 1280)
    b_w_dq, b_w_uq, b_w_o, b_q_norm = A(b_w_dq), A(b_w_uq), A(b_w_o), A(b_q_norm)
    for j in range(2):
        shared["w_dq%d" % j] = np.ascontiguousarray(
            b_w_dq[j].reshape(8, 128, 2, 128).transpose(1, 2, 0, 3)).reshape(1, 128, 2048)
        wu = b_w_uq[j].reshape(2, 128, NH, 96)
        shared["w_uq%d" % j] = np.ascontiguousarray(wu.transpose(1, 0, 2, 3)).reshape(1, 128, 3072)
        wr = np.zeros_like(wu)
        wr[..., 64:80] = wu[..., 80:96]
        wr[..., 80:96] = wu[..., 64:80]
        shared["w_uqr%d" % j] = np.ascontiguousarray(wr.transpose(1, 0, 2, 3)).reshape(1, 128, 3072)
        shared["w_o%d" % j] = np.ascontiguousarray(
            b_w_o[j].reshape(NH, 64, 4, 256).transpose(2, 1, 0, 3)).reshape(4, 64, 4096)
    shared["w_uk"] = A(w_uk)
    shared["w_uv"] = A(w_uv)
    shared["w_sT"] = np.ascontiguousarray(a_w_s.transpose(0, 3, 1, 2)).reshape(N_A, 128, 1024)
    shared["b_s"] = a_b_s.reshape(N_A, 1, 1024)
    gl = [A(ffn1_norm)[l] for l in range(DEPTH)] + [A(mix_norm)[l] for l in range(DEPTH)] + \
         [A(ffn2_norm)[l] for l in range(DEPTH)] + [A(kv_norm), A(final_norm)]
    shared["gains"] = np.ascontiguousarray(np.concatenate([_fm(g) for g in gl], axis=1))
    shared["gq"] = np.ascontiguousarray(np.concatenate([b_q_norm[j].reshape(2, 128).T for j in range(2)], axis=1))
    shared["vg"] = a_v_norm.reshape(N_A, 1, 1024)
    shared["ckvg"] = A(ckv_norm).reshape(1, 128)
    shared["ident"] = np.eye(128, dtype=f32)
    isel = np.zeros((32, 96), f32)
    isel[np.arange(32), 64 + np.arange(32)] = 1.0
    shared["isel"] = isel
    md = np.ones((128, 128), f32)
    md[64:, :64] = 0.0
    shared["mdiag"] = md
    shared["mut"] = np.triu(np.ones((128, 128), f32))
    cs, sn = _rope_tables(past + np.arange(64))
    shared["rope_tok_s"] = np.ascontiguousarray(np.concatenate([cs, cs, sn], axis=1))
    shared["rope_fm_s"] = np.ascontiguousarray(np.stack([np.concatenate([cs.T, cs.T], 0), np.concatenate([-sn.T, sn.T], 0)], axis=1))

    in_maps = []
    perms = []
    for c in range(8):
        b, par = c // 2, c % 2
        perm = np.empty(NC1, np.int64)
        perm[0::2] = np.arange(0, NC1, 2) + par
        perm[1::2] = np.arange(0, NC1, 2) + (1 - par)
        perms.append(perm)
        xl = x_prompt[b].reshape(NC1, 128, D)[perm].reshape(S, D)
        m = dict(shared)
        m["x_p"] = np.ascontiguousarray(xl.T.reshape(8, 128, S).transpose(1, 0, 2))
        m["x_s"] = np.ascontiguousarray(x_sample[c].T.reshape(8, 128, 64).transpose(1, 0, 2))
        m["c_kv"] = np.ascontiguousarray(np.concatenate([cache_ckv[c], cache_krope[c]], axis=1))
        m["modd"] = np.full((128, 128), float(par), f32)
        pos_loc = (perm[:, None] * 128 + np.arange(128)[None, :]).reshape(-1)
        cs, sn = _rope_tables(pos_loc)
        m["rope_tok_p"] = np.ascontiguousarray(np.concatenate([cs, cs, sn], axis=1))
        pos_my = ((np.arange(NMY) * 2 + par)[:, None] * 128 + np.arange(128)[None, :]).reshape(-1)
        cs, sn = _rope_tables(pos_my)
        m["rope_fm_p"] = np.ascontiguousarray(np.stack([np.concatenate([cs.T, cs.T], 0), np.concatenate([-sn.T, sn.T], 0)], axis=1))
        in_maps.append(m)

    import os as _os
    _nk = int(_os.environ.get("KCORES", "8"))
    res = run_bass_kernel_spmd(nc, in_maps[:_nk], core_ids=list(range(_nk)))
    if _nk < 8:
        res.results.extend([{k: np.zeros_like(v) for k, v in res.results[0].items()} for _ in range(8 - _nk)])
    y_prompt = np.zeros((Bn, S, D), f32)
    ckv_p = np.zeros((Bn, S, KVL), f32)
    kr_p = np.zeros((Bn, S, ROPE), f32)
    y_sample = np.zeros((8, 64, D), f32)
    ckv_s = np.zeros((8, 64, KVL), f32)
    kr_s = np.zeros((8, 64, ROPE), f32)
    av_s = np.zeros((N_A, 8, 64, D), f32)
    for c in range(8):
        b, par = c // 2, c % 2
        r = res.results[c]
        yp = np.asarray(r["y_p"]).transpose(2, 1, 0).reshape(NMY, 128, D)
        y_prompt[b].reshape(NC1, 128, D)[par::2] = yp
        kvp = np.asarray(r["o_kv_p"]).reshape(NMY, 128, 160)
        ckv_p[b].reshape(NC1, 128, KVL)[par::2] = kvp[:, :, :128]
        kr_p[b].reshape(NC1, 128, ROPE)[par::2] = kvp[:, :, 128:]
        y_sample[c] = np.asarray(r["y_s"]).transpose(2, 1, 0).reshape(64, D)
        kvs = np.asarray(r["o_kv_s"])
        ckv_s[c] = kvs[:, :128]
        kr_s[c] = kvs[:, 128:]
        av_s[:, c] = np.asarray(r["o_av_s"])
    return (y_prompt, y_sample, ckv_p, kr_p, ckv_s, kr_s, av_s)
```
